# Optimizing a Trainium2 kernel written in Bass

```python
import jax, jax.numpy as jnp
from jax import lax
import numpy as np


D_MODEL = 4096
BATCH = 4
SEQ = 4096
DEPTH = 1

HEAD_DIM = 128
ROPE_THETA = 10000.0
EPS = 1e-6
Q_BLOCK = 128
DIL_PATTERNS = ((128, 1), (512, 4), (2048, 16))
N_DIL_GROUPS = 3
A_HEADS = 8
A_WIDTH = A_HEADS * HEAD_DIM
A_QKV_WIDTH = N_DIL_GROUPS * A_WIDTH
B_HEADS = 16
B_WIDTH = B_HEADS * HEAD_DIM
A_Q0 = 0
A_K0 = A_Q0 + A_QKV_WIDTH
A_V0 = A_K0 + A_QKV_WIDTH
A_G0 = A_V0 + A_QKV_WIDTH
B_Q0 = A_G0 + A_WIDTH
B_K0 = B_Q0 + B_WIDTH
B_V0 = B_K0 + B_WIDTH
B_G0 = B_V0 + B_WIDTH
B_F0 = B_G0 + B_WIDTH
M_G0 = B_F0 + B_HEADS
IN_COLS = M_G0 + 2 * D_MODEL
FORGET_BIAS = 3.0

kernel_name = 'hybrid_dilated_forgetting_attention_block'


def rms_norm(x, g):
    xf = x.astype(jnp.float32)
    y = xf * lax.rsqrt(jnp.mean(xf * xf, axis=-1, keepdims=True) + EPS)
    return (y * g.astype(jnp.float32)).astype(x.dtype)


def rope(x, pos):
    half = HEAD_DIM // 2
    inv = ROPE_THETA ** (-jnp.arange(half, dtype=jnp.float32) / half)
    ang = pos.astype(jnp.float32)[:, None] * inv[None, :]
    bshape = (ang.shape[0],) + (1,) * (x.ndim - 3) + (half,)
    cos = jnp.cos(ang).reshape(bshape)
    sin = jnp.sin(ang).reshape(bshape)
    x1 = x[..., :half].astype(jnp.float32)
    x2 = x[..., half:].astype(jnp.float32)
    out = jnp.concatenate([x1 * cos - x2 * sin, x2 * cos + x1 * sin], axis=-1)
    return out.astype(x.dtype)


def dilated_window_attention(q, k, v, window, dilation):
    b, s, h, dh = q.shape
    span = window // dilation
    L = s // dilation
    n = -(-L // span)
    Lp = n * span

    def to_strided(t):
        t = t.reshape(b, L, dilation, h, dh).transpose(0, 2, 1, 3, 4)
        return jnp.pad(t, ((0, 0), (0, 0), (0, Lp - L), (0, 0), (0, 0)))

    def band(t):
        t = jnp.pad(t, ((0, 0), (0, 0), (span, 0), (0, 0), (0, 0)))
        t = t.reshape(b, dilation, n + 1, span, h, dh)
        return jnp.concatenate([t[:, :, :-1], t[:, :, 1:]], axis=3)

    qb = to_strided(q).reshape(b, dilation, n, span, h, dh)
    kb = band(to_strided(k))
    vb = band(to_strided(v))
    scores = jnp.einsum('brnqhd,brnkhd->brnhqk', qb.astype(jnp.float32),
                        kb.astype(jnp.float32)) * (HEAD_DIM ** -0.5)
    qi = jnp.arange(span)[:, None]
    kj = jnp.arange(2 * span)[None, :]
    dist = qi - kj + span
    blk = jnp.arange(n)[:, None, None]
    valid = (dist >= 0) & (dist <= span) & ((blk > 0) | (kj >= span))
    scores = jnp.where(valid[None, None, :, None], scores, -jnp.inf)
    m = jnp.max(scores, axis=-1, keepdims=True)
    p = jnp.exp(scores - m)
    den = jnp.sum(p, axis=-1, keepdims=True)
    o = jnp.einsum('brnhqk,brnkhd->brnqhd', (p / den).astype(v.dtype), vb)
    lse = (m + jnp.log(den))[..., 0].transpose(0, 1, 2, 4, 3)

    def from_strided(t):
        rest = t.shape[4:]
        t = t.reshape((b, dilation, Lp) + rest)[:, :, :L]
        return jnp.moveaxis(t, 1, 2).reshape((b, s) + rest)

    return from_strided(o), from_strided(lse)


def forgetting_attention(q, k, v, log_f):
    b, s, h, dh = q.shape
    nq = s // Q_BLOCK
    F = jnp.cumsum(log_f.astype(jnp.float32), axis=1)
    Ft = F.transpose(0, 2, 1)
    kf = k.astype(jnp.float32)
    kpos = jnp.arange(s)
    qb = q.reshape(b, nq, Q_BLOCK, h, dh).transpose(1, 0, 2, 3, 4)
    Fb = Ft.reshape(b, h, nq, Q_BLOCK).transpose(2, 0, 1, 3)

    def block(args):
        qi, Fi, i = args
        sc = jnp.einsum('bqhd,bkhd->bhqk', qi.astype(jnp.float32), kf) * (HEAD_DIM ** -0.5)
        sc = sc + Fi[..., None] - Ft[:, :, None, :]
        qpos = i * Q_BLOCK + jnp.arange(Q_BLOCK)
        sc = jnp.where((kpos[None, :] <= qpos[:, None])[None, None], sc, -jnp.inf)
        p = jax.nn.softmax(sc, axis=-1)
        return jnp.einsum('bhqk,bkhd->bqhd', p.astype(v.dtype), v)

    o = lax.map(block, (qb, Fb, jnp.arange(nq)))
    return o.transpose(1, 0, 2, 3, 4).reshape(b, s, h, dh)


def setup_inputs(seed: int = 0) -> dict:
    key = jax.random.key(seed)
    ks = jax.random.split(key, 16)
    f32 = jnp.float32
    x = jax.random.normal(ks[0], (BATCH, SEQ, D_MODEL), f32)
    c = jax.random.normal(ks[1], (BATCH, D_MODEL), f32)
    norm_g = 1.0 + 0.02 * jax.random.normal(ks[2], (DEPTH, D_MODEL), f32)
    w_ada = jax.random.normal(ks[3], (DEPTH, D_MODEL, 3 * D_MODEL), f32) * (0.5 * D_MODEL ** -0.5)
    b_ada = 0.01 * jax.random.normal(ks[4], (DEPTH, 3 * D_MODEL), f32)
    w_in = jax.random.normal(ks[5], (DEPTH, D_MODEL, IN_COLS), f32) * (D_MODEL ** -0.5)
    b_in = 0.01 * jax.random.normal(ks[6], (DEPTH, IN_COLS), f32)
    b_in = b_in.at[:, B_F0:M_G0].add(FORGET_BIAS)
    a_q_norm = 1.0 + 0.02 * jax.random.normal(ks[7], (DEPTH, N_DIL_GROUPS, HEAD_DIM), f32)
    a_k_norm = 1.0 + 0.02 * jax.random.normal(ks[8], (DEPTH, N_DIL_GROUPS, HEAD_DIM), f32)
    b_q_norm = 1.0 + 0.02 * jax.random.normal(ks[9], (DEPTH, HEAD_DIM), f32)
    b_k_norm = 1.0 + 0.02 * jax.random.normal(ks[10], (DEPTH, HEAD_DIM), f32)
    w_a_out = jax.random.normal(ks[11], (DEPTH, A_WIDTH, D_MODEL), f32) * (A_WIDTH ** -0.5)
    w_b_out = jax.random.normal(ks[12], (DEPTH, B_WIDTH, D_MODEL), f32) * (B_WIDTH ** -0.5)
    w_o = jax.random.normal(ks[13], (DEPTH, D_MODEL, D_MODEL), f32) * (D_MODEL ** -0.5)
    return {'x': x, 'c': c, 'norm_g': norm_g, 'w_ada': w_ada, 'b_ada': b_ada,
            'w_in': w_in, 'b_in': b_in, 'a_q_norm': a_q_norm, 'a_k_norm': a_k_norm,
            'b_q_norm': b_q_norm, 'b_k_norm': b_k_norm, 'w_a_out': w_a_out,
            'w_b_out': w_b_out, 'w_o': w_o}


def reference(x, c, norm_g, w_ada, b_ada, w_in, b_in, a_q_norm, a_k_norm,
              b_q_norm, b_k_norm, w_a_out, w_b_out, w_o):
    b, s, _ = x.shape
    pos = jnp.arange(s)
    for l in range(DEPTH):
        mod = jax.nn.silu(c) @ w_ada[l] + b_ada[l]
        shift, scale, gate = jnp.split(mod, 3, axis=-1)
        h = rms_norm(x, norm_g[l]) * (1.0 + scale[:, None, :]) + shift[:, None, :]
        z = h @ w_in[l] + b_in[l]

        qa = z[..., A_Q0:A_K0].reshape(b, s, N_DIL_GROUPS, A_HEADS, HEAD_DIM)
        ka = z[..., A_K0:A_V0].reshape(b, s, N_DIL_GROUPS, A_HEADS, HEAD_DIM)
        va = z[..., A_V0:A_G0].reshape(b, s, N_DIL_GROUPS, A_HEADS, HEAD_DIM)
        qa = rope(rms_norm(qa, a_q_norm[l][:, None, :]), pos)
        ka = rope(rms_norm(ka, a_k_norm[l][:, None, :]), pos)
        outs = []
        lses = []
        for g, (win, dil) in enumerate(DIL_PATTERNS):
            o_g, lse_g = dilated_window_attention(qa[:, :, g], ka[:, :, g], va[:, :, g], win, dil)
            outs.append(o_g)
            lses.append(lse_g)
        wts = jax.nn.softmax(jnp.stack(lses, axis=0), axis=0)
        ya = jnp.sum(wts[..., None] * jnp.stack(outs, axis=0).astype(jnp.float32), axis=0)
        ya = ya.astype(x.dtype).reshape(b, s, A_WIDTH) * jax.nn.silu(z[..., A_G0:B_Q0])

        qb = rms_norm(z[..., B_Q0:B_K0].reshape(b, s, B_HEADS, HEAD_DIM), b_q_norm[l])
        kb = rms_norm(z[..., B_K0:B_V0].reshape(b, s, B_HEADS, HEAD_DIM), b_k_norm[l])
        vb = z[..., B_V0:B_G0].reshape(b, s, B_HEADS, HEAD_DIM)
        log_f = jax.nn.log_sigmoid(z[..., B_F0:M_G0].astype(jnp.float32))
        yb = forgetting_attention(qb, kb, vb, log_f).reshape(b, s, B_WIDTH)
        yb = yb * jax.nn.silu(z[..., B_G0:B_F0])

        ga = jax.nn.sigmoid(z[..., M_G0:M_G0 + D_MODEL])
        gb = jax.nn.sigmoid(z[..., M_G0 + D_MODEL:])
        merged = ga * (ya @ w_a_out[l]) + gb * (yb @ w_b_out[l])
        out = merged @ w_o[l]
        x = x + gate[:, None, :] * out
    return x
```

```python
import contextlib
import numpy as np
import concourse.bass as bass
import concourse.mybir as mybir
from concourse.bass_utils import run_bass_kernel_spmd

F32 = mybir.dt.float32
BF16 = mybir.dt.bfloat16
ALU = mybir.AluOpType
AF = mybir.ActivationFunctionType

D = 4096
S = 4096
NCH = 8
EPS = 1e-6
SCALE = 128 ** -0.5
DILS = (1, 4, 16)
PERIOD = (1, 1, 4)
NT2 = 1 + 8 * 4 + 4 * 10
NEG = -30000.0


class T:
    __slots__ = ("name", "w", "r")

    def __init__(self, name):
        self.name = name
        self.w = None
        self.r = []


class I:
    __slots__ = ("eng", "fn", "deps", "signal", "dma", "sem", "val", "pos", "cc")


class Prog:
    ENGS = ("tensor", "vector", "scalar", "gpsimd", "sync")

    def __init__(self):
        self.streams = {e: [] for e in self.ENGS}

    def op(self, eng, fn, reads=(), writes=(), dma=False, cc=False):
        i = I()
        i.eng, i.fn, i.dma, i.signal, i.cc = eng, fn, dma, False, cc
        i.sem = i.val = None
        i.pos = len(self.streams[eng])
        deps = []
        for t in reads:
            if t.w is not None:
                deps.append(t.w)
        for t in writes:
            if t.w is not None:
                deps.append(t.w)
            deps.extend(t.r)
        best = {}
        out = []
        for d in deps:
            if d is i:
                continue
            if d.dma or d.cc:
                out.append(d)
            else:
                if d.eng == "tensor" and eng == "tensor":
                    continue
                b = best.get(d.eng)
                if b is None or d.pos > b.pos:
                    best[d.eng] = d
        out.extend(best.values())
        i.deps = out
        for d in out:
            d.signal = True
        for t in reads:
            if not (dma or cc):
                t.r = [x for x in t.r if x.dma or x.cc or x.eng != eng]
            t.r.append(i)
        for t in writes:
            t.w = i
            t.r = []
        self.streams[eng].append(i)
        return i

    def emit(self, nc, es):
        NPOOL = 12
        csem = {e: es.enter_context(nc.semaphore("c_" + e)) for e in self.ENGS}
        dsem = {e: [es.enter_context(nc.semaphore("d_%s%d" % (e, k))) for k in range(NPOOL)]
                for e in ("sync", "gpsimd")}
        ccsem = es.enter_context(nc.semaphore("ccs"))
        for e in self.ENGS:
            cnt = 0
            dcnt = [0] * NPOOL
            rr = 0
            ccn = 0
            for i in self.streams[e]:
                if i.cc:
                    ccn += 1
                    i.sem, i.val = ccsem, ccn
                    i.signal = True
                elif i.dma:
                    k = rr % NPOOL
                    rr += 1
                    dcnt[k] += 16
                    i.sem, i.val = dsem[e][k], dcnt[k]
                    i.signal = True
                elif i.signal:
                    cnt += 1
                    i.sem, i.val = csem[e], cnt
        block = es.enter_context(nc.Block())

        def run(e):
            def body(eng):
                waited = {}
                for i in self.streams[e]:
                    need = {}
                    for d in i.deps:
                        k = id(d.sem)
                        if waited.get(k, 0) >= d.val:
                            continue
                        if k not in need or need[k][1] < d.val:
                            need[k] = (d.sem, d.val)
                    for k, (s, v) in need.items():
                        eng.wait_ge(s, v)
                        waited[k] = v
                    ins = i.fn(eng)
                    if i.cc:
                        ins.then_inc(i.sem)
                    elif i.dma:
                        ins.then_inc(i.sem, 16)
                    elif i.signal:
                        ins.then_inc(i.sem, 1)
                if e in dsem:
                    last = {}
                    for i in self.streams[e]:
                        if i.dma or i.cc:
                            last[id(i.sem)] = (i.sem, i.val)
                    for s, v in last.values():
                        eng.wait_ge(s, v)
            return body

        block.tensor(run("tensor"))
        block.vector(run("vector"))
        block.scalar(run("scalar"))
        block.gpsimd(run("gpsimd"))
        block.sync(run("sync"))


def build(upto=9, dbg=False):
    nc = bass.Bass("TRN2", target_bir_lowering=False)
    es = contextlib.ExitStack()
    P = Prog()

    def din(name, shape, dt=F32):
        return nc.dram_tensor(name, list(shape), dt, kind="ExternalInput").ap()

    x_d = din("x", [S, D])
    xo_d = din("xo", [S // 2, D])
    cT_d = din("cT", [128, 32])
    wada_d = din("wada", [D, 3 * D])
    badaT_d = din("badaT", [128, 96])
    gT_d = din("gT", [128, 32])
    w2_d = din("w2", [D, NT2 * 128])
    b2T_d = din("b2T", [128, NT2])
    wmg_d = din("wmg", [D, 2 * D])
    bmgT_d = din("bmgT", [128, 64])
    wab_d = din("wab", [24 * 128, D])
    wo_d = din("wo", [D, D])
    gains_d = din("gains", [128, 8])
    cs_d = din("cs", [2, 128, S])
    msk_d = din("msk", [128, 2])
    identF_d = din("identF", [128, 128])
    selF_d = din("selF", [8, 8 * 128])
    maskT_d = din("maskT", [128, 128])
    maskA_d = din("maskA", [128, 256])
    out_d = nc.dram_tensor("out", [S // 2, D], F32, kind="ExternalOutput").ap()

    if dbg:
        dbg_mod = nc.dram_tensor("dbg_mod", [128, 96], F32, kind="ExternalOutput").ap()
        dbg_F = nc.dram_tensor("dbg_F", [8, S], F32, kind="ExternalOutput").ap()
        dbg_Ftok = nc.dram_tensor("dbg_Ftok", [128, 256], F32, kind="ExternalOutput").ap()
        dbg_ys = nc.dram_tensor("dbg_ys", [12 * 128, S], BF16, kind="ExternalOutput").ap()
        dbg_h = nc.dram_tensor("dbg_h", [128, 32 * 512], BF16, kind="ExternalOutput").ap()
    hT_d = nc.dram_tensor("hT_d", [NCH, 128, 32 * 512], BF16)
    hTo_d = nc.dram_tensor("hTo_d", [NCH // 2, 128, 32 * 512], BF16)
    ys_p = [[nc.dram_tensor("ys_%d_%d" % (u, hf), [128, S // 2], BF16) for hf in range(2)] for u in range(12)]
    ya_p = [[nc.dram_tensor("ya_%d_%d" % (u, hf), [256, S // 2], BF16) for hf in range(2)] for u in range(12)]
    ya_t = [[T("ya_%d_%d" % (u, hf)) for hf in range(2)] for u in range(12)]

    def ys_dst(u, c):
        return ys_p[u][c // 4].ap()[:, (c % 4) * 512:(c % 4 + 1) * 512]

    def exchange(u, hf):
        P.op("gpsimd", lambda e: e.collective_compute("AllGather", ALU.bypass, replica_groups=[[0, 1], [2, 3], [4, 5], [6, 7]],
                                                      ins=[ys_p[u][hf].ap().opt()], outs=[ya_p[u][hf].ap().opt()]),
             reads=[ys_t[(u, c)] for c in range(4 * hf, 4 * hf + 4)], writes=[ya_t[u][hf]], cc=True)

    def sb(name, shape, dt):
        return es.enter_context(nc.sbuf_tensor(name + "_s", list(shape), dt))

    def pst(name, shape, dt):
        return es.enter_context(nc.psum_tensor(name, list(shape), dt))

    wslot = [sb("wslot%d" % k, [128, 32, 256], BF16) for k in range(3)]
    wslot_t = [T("wslot%d" % k) for k in range(3)]
    hslot = [sb("hslot%d" % k, [128, 32, 512], BF16) for k in range(2)]
    hslot_t = [T("hslot%d" % k) for k in range(2)]
    big = [sb("big%d" % k, [128, 4096], F32) for k in range(2)]
    big_t = [T("big%d" % k) for k in range(2)]
    arena = sb("arena", [128, 18 * 1024], BF16)
    arena2 = sb("arena2", [128, 6 * 1024], BF16)
    off = [0]

    def carve(nelem_bf16, ar=None, lim=18 * 1024):
        ar = arena if ar is None else ar
        a = ar[:, off[0]:off[0] + nelem_bf16]
        off[0] += nelem_bf16
        assert off[0] <= lim, off[0]
        return a

    def c2(nelem_bf16):
        return carve(nelem_bf16, arena2, 6 * 1024)

    kT = carve(4096)
    Vt = carve(4096).rearrange("p (t d) -> p t d", d=128)
    gA = carve(2048)
    qp = carve(2048)
    vTp = carve(2048)
    kT_t, Vt_t, gA_t, qp_t, vTp_t = T("kT"), T("Vt"), T("gA"), T("qp"), T("vTp")
    arena_t = [kT_t, Vt_t, gA_t, qp_t, vTp_t]
    off[0] = 0
    ych = carve(24 * 512).rearrange("p (k t) -> p k t", t=512)
    wabs = carve(24 * 256).rearrange("p (k n) -> p k n", n=256)
    ych_t, wabs_t = T("ych"), T("wabs")
    mT_t = T("mT")
    a2_t = T("arena2")
    off[0] = 0
    Fb = [c2(1024).bitcast(F32)]
    tS = [c2(1024).bitcast(F32)]
    selF = c2(2048).bitcast(F32)[0:8, :]
    off[0] = 0
    cst = [c2(2048).bitcast(F32).rearrange("p (a t) -> p a t", a=2)]
    qn = [c2(1024).bitcast(F32), None]
    rt = [c2(1024).bitcast(F32), None]
    off[0] = 0
    xo_s = [c2(1024).bitcast(F32).rearrange("p (t n) -> p t n", n=128) for k in range(2)]
    ob_s = [c2(1024).bitcast(F32).rearrange("p (t n) -> p t n", n=128) for k in range(2)]
    Fb_t = [a2_t]
    tS_t = [a2_t]
    cst_t = [T("cst")]
    qn_t = [T("qn0"), None]
    rt_t = [T("rt0"), None]
    xo_t = [T("xo0"), T("xo1")]
    ob_t = [T("ob0"), T("ob1")]

    identF = sb("identF", [128, 128], F32)
    identB = sb("identB", [128, 128], BF16)
    onesB = sb("onesB", [128, 128], BF16)
    maskT = sb("maskT", [128, 128], F32)
    maskA = sb("maskA", [128, 256], BF16)
    cT = sb("cT", [128, 32], F32)
    scB = sb("scB", [128, 32], BF16)
    modT = sb("modT", [128, 96], F32)
    badaT = sb("badaT", [128, 96], F32)
    gT = sb("gT", [128, 32], F32)
    sc1 = sb("sc1", [128, 32], F32)
    b2T = sb("b2T", [128, NT2], F32)
    bmgT = sb("bmgT", [128, 64], F32)
    gains = sb("gains", [128, 8], F32)
    msk = sb("msk", [128, 2], F32)
    Ftok = sb("Ftok", [128, 32 * 8], F32)
    zt = [sb("zt0", [128, 512], F32)]
    zt_t = [T("zt0")]
    sqb = [sb("sqb0", [128, 512], BF16)]
    sqb_t = [T("sqb0")]
    sd = [sb("sd0", [128, 512], F32)]
    sd_t = [T("sd0")]
    qc = [sb("qc%d" % k, [128, 512], BF16) for k in range(1)]
    qc_t = [T("qc%d" % k) for k in range(1)]
    gc = [sb("gc%d" % k, [128, 512], BF16) for k in range(1)]
    gc_t = [T("gc%d" % k) for k in range(1)]
    vTc = [sb("vTc%d" % k, [128, 512], BF16) for k in range(1)]
    vTc_t = [T("vTc%d" % k) for k in range(1)]
    PT = [sb("PT%d" % k, [128, 512], BF16) for k in range(2)]
    PT_t = [T("PT%d" % k) for k in range(2)]
    ybc = [sb("ybc%d" % k, [128, 512], BF16) for k in range(1)]
    ybc_t = [T("ybc%d" % k) for k in range(1)]
    ss = sb("ss", [128, 4], F32)
    ss_t = T("ss")
    const_t = T("const")
    mod_t = T("mod")
    Ftok_t = T("Ftok")

    bank = [pst("bank%d" % k, [128, 512], F32) for k in range(7)]
    bank_t = [T("bank%d" % k) for k in range(7)]
    pbB = pst("pbB", [128, 1024], BF16)
    pbB_t = T("pbB")
    IN = [0, 1, 2]
    AUX, SB_, OB_, DB_ = 3, 4, 5, 6

    rr = {"in": 0, "pt": 0, "w": 0, "h": 0}
    ys_t = {}

    def finish():
        if dbg:
            P.op("sync", lambda e: e.dma_start(out=dbg_mod, in_=modT[:]), reads=[mod_t], writes=[T("d1")], dma=True)
            if upto >= 1:
                P.op("sync", lambda e: e.dma_start(out=dbg_h, in_=hT_d[0]), reads=[hd_t[("a", 0)]], writes=[T("d2")], dma=True)
            if upto >= 3:
                for u in range(12):
                    for hf in range(2):
                        P.op("sync", lambda e, u=u, hf=hf: e.dma_start(out=dbg_ys[u * 128:(u + 1) * 128, hf * 2048:(hf + 1) * 2048],
                                                                       in_=ys_p[u][hf].ap()),
                             reads=[ys_t[(u, c)] for c in range(4 * hf, 4 * hf + 4) if (u, c) in ys_t], writes=[T("d3")], dma=True)
        P.emit(nc, es)
        es.close()
        return nc

    def nxt(k, n):
        v = rr[k] % n
        rr[k] += 1
        return v

    def ld(dst, src, t, eng="sync"):
        P.op(eng, lambda e: e.dma_start(out=dst, in_=src), writes=[t], dma=True)

    for dst, src in ((identF[:], identF_d), (maskT[:], maskT_d), (zt[0][:, 0:256], maskA_d),
                     (cT[:], cT_d), (badaT[:], badaT_d), (gT[:], gT_d),
                     (b2T[:], b2T_d), (bmgT[:], bmgT_d), (gains[:], gains_d), (msk[:], msk_d)):
        ld(dst, src, const_t)
    P.op("vector", lambda e: e.tensor_copy(out=identB[:], in_=identF[:]), reads=[const_t], writes=[const_t])
    P.op("vector", lambda e: e.memset(onesB[:], 1.0), writes=[const_t])
    P.op("vector", lambda e: e.tensor_copy(out=maskA[:], in_=zt[0][:, 0:256]), reads=[const_t], writes=[const_t, zt_t[0]])
    P.op("scalar", lambda e: e.activation(out=scB[:], in_=cT[:], func=AF.Silu), reads=[const_t], writes=[const_t])

    wada_v = wada_d.rearrange("(k p) n -> p k n", p=128)

    def ph0(cg):
        ws = nxt("w", 3)
        P.op("gpsimd", lambda e: e.dma_start(out=wslot[ws][:], in_=wada_v[:, :, cg * 256:(cg + 1) * 256]),
             writes=[wslot_t[ws]], dma=True)
        for j in range(2):
            ct = cg * 2 + j
            for k in range(32):
                P.op("tensor", lambda e, j=j, k=k, ct=ct: e.matmul(
                    bank[AUX][:, ct:ct + 1], lhsT=wslot[ws][:, k, j * 128:(j + 1) * 128], rhs=scB[:, k:k + 1],
                    start=(k == 0), stop=(k == 31)),
                    reads=[wslot_t[ws], const_t], writes=[bank_t[AUX]])
    for cg in range(48):
        ph0(cg)
    P.op("vector", lambda e: e.tensor_tensor(out=modT[:], in0=bank[AUX][:, 0:96], in1=badaT[:], op=ALU.add),
         reads=[bank_t[AUX], const_t], writes=[mod_t])
    P.op("vector", lambda e: e.scalar_tensor_tensor(out=sc1[:], in0=modT[:, 32:64], scalar=1.0, in1=gT[:],
                                                    op0=ALU.add, op1=ALU.mult),
         reads=[mod_t, const_t], writes=[mod_t])
    shiftT = modT[:, 0:32]
    gateT = modT[:, 64:96]

    if upto <= 0:
        return finish()
    junk = arena[:, 0:4096]
    junk_t = kT_t
    hd_t = {}
    for ch in range(NCH):
        hd_t[("a", ch)] = T("hTd%d" % ch)
    for ch in range(NCH // 2):
        hd_t[("o", ch)] = T("hTod%d" % ch)

    def h_tile(src, dst, key, tt):
        xb = tt % 2
        xt = big[xb]
        tl = tt % 4
        ch = tt // 4
        hs = ch % 2
        P.op("sync", lambda e: e.dma_start(out=xt[:], in_=src[tt * 128:(tt + 1) * 128, :]),
             writes=[big_t[xb]], dma=True)
        P.op("scalar", lambda e: e.activation(out=junk, in_=xt[:], func=AF.Square, accum_out=ss[:, 0:1]),
             reads=[big_t[xb]], writes=[junk_t, ss_t])
        P.op("scalar", lambda e: e.activation(out=ss[:, 1:2], in_=ss[:, 0:1], func=AF.Sqrt, scale=1.0 / D, bias=EPS),
             reads=[ss_t], writes=[ss_t])
        P.op("vector", lambda e: e.reciprocal(out=ss[:, 2:3], in_=ss[:, 1:2]), reads=[ss_t], writes=[ss_t])
        P.op("vector", lambda e: e.tensor_scalar(out=xt[:], in0=xt[:], scalar1=ss[:, 2:3], scalar2=None, op0=ALU.mult),
             reads=[ss_t, big_t[xb]], writes=[big_t[xb]])
        for kg in range(8):
            bk = IN[nxt("in", 3)]
            for q in range(4):
                kc = kg * 4 + q
                P.op("tensor", lambda e, kc=kc, q=q, bk=bk: e.transpose(
                    out=bank[bk][:, q * 128:(q + 1) * 128], in_=xt[:, kc * 128:(kc + 1) * 128], identity=identF[:]),
                    reads=[big_t[xb], const_t], writes=[bank_t[bk]])
            for q in range(4):
                kc = kg * 4 + q
                if q % 2 == 0:
                    P.op("vector", lambda e, kc=kc, q=q, bk=bk: e.tensor_scalar(
                        out=hslot[hs][:, kc, tl * 128:(tl + 1) * 128], in0=bank[bk][:, q * 128:(q + 1) * 128],
                        scalar1=sc1[:, kc:kc + 1], scalar2=shiftT[:, kc:kc + 1], op0=ALU.mult, op1=ALU.add),
                        reads=[bank_t[bk], mod_t], writes=[hslot_t[hs]])
                else:
                    P.op("scalar", lambda e, kc=kc, q=q, bk=bk: e.activation(
                        out=hslot[hs][:, kc, tl * 128:(tl + 1) * 128], in_=bank[bk][:, q * 128:(q + 1) * 128],
                        func=AF.Identity, scale=sc1[:, kc:kc + 1], bias=shiftT[:, kc:kc + 1]),
                        reads=[bank_t[bk], mod_t], writes=[hslot_t[hs]])
        if tl == 3:
            P.op("sync", lambda e: e.dma_start(out=dst[ch], in_=hslot[hs][:].rearrange("p k t -> p (k t)")),
                 reads=[hslot_t[hs]], writes=[hd_t[(key, ch)]], dma=True)

    for tt in range(32):
        h_tile(x_d, hT_d, "a", tt)
    for tt in range(16):
        h_tile(xo_d, hTo_d, "o", tt)

    if upto <= 1:
        return finish()
    w2_v = w2_d.rearrange("(k p) n -> p k n", p=128)
    wmg_v = wmg_d.rearrange("(k p) n -> p k n", p=128)
    wo_v = wo_d.rearrange("(k p) n -> p k n", p=128)

    def load_w(view, col0, ncols):
        ws = nxt("w", 3)
        P.op("gpsimd", lambda e: e.dma_start(out=wslot[ws][:, :, 0:ncols], in_=view[:, :, col0:col0 + ncols]),
             writes=[wslot_t[ws]], dma=True)
        return ws

    def load_h(src, key, ch, hs=None):
        if hs is None:
            hs = nxt("h", 2)
        P.op("sync", lambda e: e.dma_start(out=hslot[hs][:].rearrange("p k t -> p (k t)"), in_=src[ch]),
             reads=[hd_t[(key, ch)]], writes=[hslot_t[hs]], dma=True)
        return hs

    def mm_tile(ws, wj, hs):
        bk = IN[nxt("in", 3)]
        for k in range(32):
            P.op("tensor", lambda e, k=k: e.matmul(bank[bk][:], lhsT=wslot[ws][:, k, wj * 128:(wj + 1) * 128],
                                                   rhs=hslot[hs][:, k, :], start=(k == 0), stop=(k == 31)),
                 reads=[wslot_t[ws], hslot_t[hs]], writes=[bank_t[bk]])
        return bk

    def qk_norm(bk, bcol, gcol, out_ap, out_t):
        z = 0
        P.op("scalar", lambda e: e.activation(out=zt[z][:], in_=bank[bk][:], func=AF.Identity, bias=b2T[:, bcol:bcol + 1]),
             reads=[bank_t[bk], const_t], writes=[zt_t[z]])
        P.op("scalar", lambda e: e.activation(out=sqb[z][:], in_=bank[bk][:], func=AF.Square, bias=b2T[:, bcol:bcol + 1]),
             reads=[bank_t[bk], const_t], writes=[sqb_t[z]])
        P.op("tensor", lambda e: e.matmul(bank[AUX][:], lhsT=onesB[:], rhs=sqb[z][:], start=True, stop=True),
             reads=[sqb_t[z], const_t], writes=[bank_t[AUX]])
        P.op("scalar", lambda e: e.activation(out=sd[z][:], in_=bank[AUX][:], func=AF.Sqrt, scale=1.0 / 128, bias=EPS),
             reads=[bank_t[AUX]], writes=[sd_t[z]])
        P.op("vector", lambda e: e.reciprocal(out=sd[z][:], in_=sd[z][:]), reads=[sd_t[z]], writes=[sd_t[z]])
        P.op("vector", lambda e: e.scalar_tensor_tensor(out=out_ap, in0=zt[z][:], scalar=gains[:, gcol:gcol + 1],
                                                        in1=sd[z][:], op0=ALU.mult, op1=ALU.mult),
             reads=[zt_t[z], sd_t[z], const_t], writes=[out_t])

    FT = big[0]
    lf = big[1]

    def f_chunk(ws, c):
        hs = load_h(hT_d, "a", c)
        bk = mm_tile(ws, 0, hs)
        cs_ = slice(c * 512, (c + 1) * 512)
        P.op("scalar", lambda e: e.activation(out=lf[0:8, cs_], in_=bank[bk][0:8, :], func=AF.Identity, bias=b2T[0:8, 0:1]),
             reads=[bank_t[bk], const_t], writes=[big_t[1]])
        P.op("scalar", lambda e: e.activation(out=FT[0:8, cs_], in_=lf[0:8, cs_], func=AF.Abs),
             reads=[big_t[1]], writes=[big_t[0]])
        P.op("scalar", lambda e: e.activation(out=FT[0:8, cs_], in_=FT[0:8, cs_], func=AF.Exp, scale=-1.0),
             reads=[big_t[0]], writes=[big_t[0]])
        P.op("scalar", lambda e: e.activation(out=FT[0:8, cs_], in_=FT[0:8, cs_], func=AF.Ln, bias=1.0),
             reads=[big_t[0]], writes=[big_t[0]])
        P.op("vector", lambda e: e.tensor_single_scalar(out=lf[0:8, cs_], in_=lf[0:8, cs_], scalar=0.0, op=ALU.min),
             reads=[big_t[1]], writes=[big_t[1]])
        P.op("vector", lambda e: e.tensor_tensor(out=lf[0:8, cs_], in0=lf[0:8, cs_], in1=FT[0:8, cs_], op=ALU.subtract),
             reads=[big_t[1], big_t[0]], writes=[big_t[1]])

    ws0 = load_w(w2_v, 0, 128)
    for c in range(NCH):
        f_chunk(ws0, c)
    P.op("vector", lambda e: e.tensor_tensor_scan(out=FT[0:8, :], data0=lf[0:8, :], data1=lf[0:8, :], initial=0.0,
                                                  op0=ALU.add, op1=ALU.bypass),
         reads=[big_t[1]], writes=[big_t[0]])
    for tt in range(32):
        P.op("tensor", lambda e, tt=tt: e.transpose(out=bank[AUX][:, 0:8], in_=FT[0:8, tt * 128:(tt + 1) * 128],
                                                    identity=identF[0:8, 0:8]),
             reads=[big_t[0], const_t], writes=[bank_t[AUX]])
        P.op("scalar", lambda e, tt=tt: e.mul(out=Ftok[:, tt * 8:(tt + 1) * 8], in_=bank[AUX][:, 0:8], mul=-1.0),
             reads=[bank_t[AUX]], writes=[Ftok_t])
    P.op("sync", lambda e: e.dma_start(out=selF, in_=selF_d), writes=[a2_t], dma=True)
    if dbg:
        P.op("sync", lambda e: e.dma_start(out=dbg_F, in_=FT[0:8, :]), reads=[big_t[0]], writes=[T("d4")], dma=True)
        P.op("sync", lambda e: e.dma_start(out=dbg_Ftok, in_=Ftok[:]), reads=[Ftok_t], writes=[T("d5")], dma=True)
    if upto <= 2:
        return finish()
    tile_box = [1]

    def b_chunk(hb, t0, wsA, wsB, c):
        hs = load_h(hT_d, "a", c)
        cs_ = slice(c * 512, (c + 1) * 512)
        bk = mm_tile(wsA, 0, hs)
        qk_norm(bk, t0, 0, qc[0][:], qc_t[0])
        bk = mm_tile(wsA, 1, hs)
        qk_norm(bk, t0 + 1, 1, kT[:, cs_], kT_t)
        bkv = mm_tile(wsB, 0, hs)
        P.op("scalar", lambda e: e.activation(out=vTc[0][:], in_=bank[bkv][:], func=AF.Identity, bias=b2T[:, t0 + 2:t0 + 3]),
             reads=[bank_t[bkv], const_t], writes=[vTc_t[0]])
        for q in range(4):
            P.op("tensor", lambda e, q=q: e.transpose(out=pbB[:, q * 128:(q + 1) * 128],
                                                      in_=vTc[0][:, q * 128:(q + 1) * 128], identity=identB[:]),
                 reads=[vTc_t[0], const_t], writes=[pbB_t])
        P.op("vector", lambda e: e.tensor_copy(out=Vt[:, 4 * c:4 * c + 4, :],
                                               in_=pbB[:, 0:512].rearrange("p (t d) -> p t d", d=128)),
             reads=[pbB_t], writes=[Vt_t])
        bkg = mm_tile(wsB, 1, hs)
        P.op("scalar", lambda e: e.activation(out=gc[0][:], in_=bank[bkg][:], func=AF.Silu, bias=b2T[:, t0 + 3:t0 + 4]),
             reads=[bank_t[bkg], const_t], writes=[gc_t[0]])
        P.op("tensor", lambda e: e.matmul(bank[AUX][:], lhsT=selF[0:8, hb * 128:(hb + 1) * 128], rhs=FT[0:8, cs_],
                                          start=True, stop=True),
             reads=[big_t[0], a2_t], writes=[bank_t[AUX]])
        P.op("vector", lambda e: e.tensor_copy(out=Fb[0], in_=bank[AUX][:]), reads=[bank_t[AUX]], writes=[Fb_t[0]])
        nkt = 4 * c + 4
        for kt in range(nkt):
            j = kt - 4 * c
            q0 = 128 * j if j > 0 else 0
            P.op("tensor", lambda e, kt=kt, q0=q0: e.matmul(bank[SB_][:, q0:512], lhsT=kT[:, kt * 128:(kt + 1) * 128],
                                                           rhs=qc[0][:, q0:512], start=True, stop=True),
                 reads=[kT_t, qc_t[0]], writes=[bank_t[SB_]])
            P.op("vector", lambda e, q0=q0: e.scalar_tensor_tensor(
                out=tS[0][:, q0:512], in0=bank[SB_][:, q0:512], scalar=SCALE, in1=Fb[0][:, q0:512], op0=ALU.mult, op1=ALU.add),
                reads=[bank_t[SB_], Fb_t[0]], writes=[tS_t[0]])
            if j >= 0:
                P.op("vector", lambda e, q0=q0: e.tensor_tensor(out=tS[0][:, q0:q0 + 128], in0=tS[0][:, q0:q0 + 128],
                                                                in1=maskT[:], op=ALU.add),
                     reads=[tS_t[0], const_t], writes=[tS_t[0]])
            pi = nxt("pt", 2)
            P.op("scalar", lambda e, pi=pi, kt=kt, q0=q0: e.activation(
                out=PT[pi][:, q0:512], in_=tS[0][:, q0:512], func=AF.Exp, bias=Ftok[:, kt * 8 + hb:kt * 8 + hb + 1]),
                reads=[tS_t[0], Ftok_t], writes=[PT_t[pi]])
            P.op("tensor", lambda e, pi=pi, kt=kt, q0=q0: e.matmul(bank[OB_][:, q0:512], lhsT=Vt[:, kt, :], rhs=PT[pi][:, q0:512],
                                                                  start=(kt == 0), stop=(kt == nkt - 1)),
                 reads=[Vt_t, PT_t[pi]], writes=[bank_t[OB_]])
            P.op("tensor", lambda e, pi=pi, kt=kt, q0=q0: e.matmul(bank[DB_][:, q0:512], lhsT=onesB[:], rhs=PT[pi][:, q0:512],
                                                                  start=(kt == 0), stop=(kt == nkt - 1)),
                 reads=[const_t, PT_t[pi]], writes=[bank_t[DB_]])
        P.op("vector", lambda e: e.reciprocal(out=zt[0][:], in_=bank[DB_][:]), reads=[bank_t[DB_]], writes=[zt_t[0]])
        P.op("vector", lambda e: e.tensor_tensor(out=zt[0][:], in0=bank[OB_][:], in1=zt[0][:], op=ALU.mult),
             reads=[bank_t[OB_], zt_t[0]], writes=[zt_t[0]])
        P.op("vector", lambda e: e.tensor_tensor(out=ybc[0][:], in0=zt[0][:], in1=gc[0][:], op=ALU.mult),
             reads=[zt_t[0], gc_t[0]], writes=[ybc_t[0]])
        yt = T("ys")
        ys_t[(hb, c)] = yt
        P.op("sync", lambda e: e.dma_start(out=ys_dst(hb, c), in_=ybc[0][:]),
             reads=[ybc_t[0]], writes=[yt], dma=True)
        if c % 4 == 3:
            exchange(hb, c // 4)

    for hb in range(8):
        t0 = tile_box[0]
        tile_box[0] += 4
        wsA = load_w(w2_v, t0 * 128, 256)
        wsB = load_w(w2_v, (t0 + 2) * 128, 256)
        for c in range(NCH):
            b_chunk(hb, t0, wsA, wsB, c)

    if upto <= 3:
        return finish()
    num, den = big[0], big[1]

    def rope(dil, src_ap, src_t, dst_ap, dst_t):
        P.op("scalar", lambda e: e.copy(out=rt[0][0:64, :], in_=src_ap[64:128, :]), reads=[src_t], writes=[rt_t[0]])
        P.op("scalar", lambda e: e.copy(out=rt[0][64:128, :], in_=src_ap[0:64, :]), reads=[src_t], writes=[rt_t[0]])
        P.op("vector", lambda e: e.tensor_tensor(out=rt[0], in0=rt[0], in1=cst[0][:, 1, :], op=ALU.mult),
             reads=[rt_t[0], cst_t[0]], writes=[rt_t[0]])
        P.op("vector", lambda e: e.tensor_tensor(out=src_ap, in0=src_ap, in1=cst[0][:, 0, :], op=ALU.mult),
             reads=[src_t, cst_t[0]], writes=[src_t])
        P.op("vector", lambda e: e.tensor_tensor(
            out=dst_ap, in0=src_ap.rearrange("p (m r) -> p m r", r=dil),
            in1=rt[0].rearrange("p (m r) -> p m r", r=dil), op=ALU.add),
            reads=[src_t, rt_t[0]], writes=[dst_t])

    def a_block(g, dil, nblk, kcm, qpm, r, nbl, n):
        ms = [m for m in (n - 1, n) if m >= 0]
        for mi, m in enumerate(ms):
            P.op("tensor", lambda e, m=m: e.matmul(
                bank[SB_][:, 0:128], lhsT=kcm[:, r, m * 128:(m + 1) * 128],
                rhs=qpm[:, r, nbl * 128:(nbl + 1) * 128], start=True, stop=True),
                reads=[kT_t, qp_t], writes=[bank_t[SB_]])
            pi = nxt("pt", 2)
            P.op("scalar", lambda e, pi=pi: e.activation(out=PT[pi][:, 0:128], in_=bank[SB_][:, 0:128], func=AF.Exp, scale=SCALE),
                 reads=[bank_t[SB_]], writes=[PT_t[pi]])
            mo = 0 if m == n else 128
            P.op("vector", lambda e, pi=pi, mo=mo: e.tensor_tensor(
                out=PT[pi][:, 0:128], in0=PT[pi][:, 0:128], in1=maskA[:, mo:mo + 128], op=ALU.mult),
                reads=[PT_t[pi], const_t], writes=[PT_t[pi]])
            P.op("tensor", lambda e, pi=pi, m=m, mi=mi: e.matmul(
                bank[OB_][:, 0:128], lhsT=Vt[:, r * nblk + m, :], rhs=PT[pi][:, 0:128],
                start=(mi == 0), stop=(mi == len(ms) - 1)),
                reads=[Vt_t, PT_t[pi]], writes=[bank_t[OB_]])
            P.op("tensor", lambda e, pi=pi, mi=mi: e.matmul(
                bank[DB_][:, 0:128], lhsT=onesB[:], rhs=PT[pi][:, 0:128],
                start=(mi == 0), stop=(mi == len(ms) - 1)),
                reads=[const_t, PT_t[pi]], writes=[bank_t[DB_]])
        numv = num[:].rearrange("p (l r) -> p r l", r=dil)[:, r, n * 128:(n + 1) * 128]
        denv = den[:].rearrange("p (l r) -> p r l", r=dil)[:, r, n * 128:(n + 1) * 128]
        if g == 0:
            P.op("vector", lambda e: e.tensor_copy(out=numv, in_=bank[OB_][:, 0:128]),
                 reads=[bank_t[OB_]], writes=[big_t[0]])
            P.op("scalar", lambda e: e.copy(out=denv, in_=bank[DB_][:, 0:128]),
                 reads=[bank_t[DB_]], writes=[big_t[1]])
        else:
            P.op("vector", lambda e: e.tensor_tensor(out=numv, in0=bank[OB_][:, 0:128], in1=numv, op=ALU.add),
                 reads=[bank_t[OB_], big_t[0]], writes=[big_t[0]])
            P.op("vector", lambda e: e.tensor_tensor(out=denv, in0=bank[DB_][:, 0:128], in1=denv, op=ALU.add),
                 reads=[bank_t[DB_], big_t[1]], writes=[big_t[1]])

    def a_combine(ha, c):
        cs_ = slice(c * 512, (c + 1) * 512)
        gs_ = slice((c % 4) * 512, (c % 4 + 1) * 512)
        P.op("vector", lambda e: e.reciprocal(out=den[:, cs_], in_=den[:, cs_]), reads=[big_t[1]], writes=[big_t[1]])
        P.op("vector", lambda e: e.tensor_tensor(out=num[:, cs_], in0=num[:, cs_], in1=den[:, cs_], op=ALU.mult),
             reads=[big_t[0], big_t[1]], writes=[big_t[0]])
        P.op("vector", lambda e: e.tensor_tensor(out=ybc[0][:], in0=num[:, cs_], in1=gA[:, gs_], op=ALU.mult),
             reads=[big_t[0], gA_t], writes=[ybc_t[0]])
        yt = T("ys")
        ys_t[(8 + ha, c)] = yt
        P.op("sync", lambda e: e.dma_start(out=ys_dst(8 + ha, c), in_=ybc[0][:]),
             reads=[ybc_t[0]], writes=[yt], dma=True)
        if c % 4 == 3:
            exchange(8 + ha, c // 4)

    def a_group(ha, g, t0):
        dil, per = DILS[g], PERIOD[g]
        L = S // dil
        W = 512 // dil
        LP = per * W
        NB = LP // 128
        nblk = L // 128
        ntile = 4 if g == 2 else 3
        wsA = load_w(w2_v, t0 * 128, 256)
        wsB = load_w(w2_v, (t0 + 2) * 128, 128 * (ntile - 2))
        kcm = kT.rearrange("p (r l) -> p r l", r=dil)
        qpm = qp[:, 0:dil * LP].rearrange("p (r l) -> p r l", r=dil)
        vpm = vTp[:, 0:dil * LP].rearrange("p (r l) -> p r l", r=dil)

        def a_chunk(c):
            hs = load_h(hT_d, "a", c)
            cc = c % per
            pi_ = c // per
            cs_ = slice(c * 512, (c + 1) * 512)
            P.op("sync", lambda e: e.dma_start(out=cst[0], in_=cs_d[:, :, cs_].rearrange("a p t -> p a t")),
                 writes=[cst_t[0]], dma=True)
            bk = mm_tile(wsA, 0, hs)
            qk_norm(bk, t0, 2 + g, qn[0], qn_t[0])
            rope(dil, qn[0], qn_t[0], qpm[:, :, cc * W:(cc + 1) * W].rearrange("p r m -> p m r"), qp_t)
            bk = mm_tile(wsA, 1, hs)
            qk_norm(bk, t0 + 1, 5 + g, qn[0], qn_t[0])
            rope(dil, qn[0], qn_t[0], kcm[:, :, c * W:(c + 1) * W].rearrange("p r m -> p m r"), kT_t)
            bkv = mm_tile(wsB, 0, hs)
            P.op("scalar", lambda e: e.activation(
                out=vpm[:, :, cc * W:(cc + 1) * W].rearrange("p r m -> p m r"),
                in_=bank[bkv][:].rearrange("p (m r) -> p m r", r=dil), func=AF.Identity, bias=b2T[:, t0 + 2:t0 + 3]),
                reads=[bank_t[bkv], const_t], writes=[vTp_t])
            if g == 2:
                bkg = mm_tile(wsB, 1, hs)
                gs_ = slice(cc * 512, (cc + 1) * 512)
                P.op("scalar", lambda e: e.activation(out=gA[:, gs_], in_=bank[bkg][:], func=AF.Silu, bias=b2T[:, t0 + 3:t0 + 4]),
                     reads=[bank_t[bkg], const_t], writes=[gA_t])
            if cc != per - 1:
                return
            nb0 = pi_ * NB
            for r in range(dil):
                for nbl in range(NB):
                    n = nb0 + nbl
                    P.op("tensor", lambda e, r=r, nbl=nbl: e.transpose(
                        out=pbB[:, 0:128], in_=vpm[:, r, nbl * 128:(nbl + 1) * 128], identity=identB[:]),
                        reads=[vTp_t, const_t], writes=[pbB_t])
                    P.op("vector", lambda e, r=r, n=n: e.tensor_copy(out=Vt[:, r * nblk + n, :], in_=pbB[:, 0:128]),
                         reads=[pbB_t], writes=[Vt_t])
            for r in range(dil):
                for nbl in range(NB):
                    a_block(g, dil, nblk, kcm, qpm, r, nbl, nb0 + nbl)
            if g == 2:
                for c2_ in range(4 * pi_, 4 * pi_ + 4):
                    a_combine(ha, c2_)

        for c in range(NCH):
            a_chunk(c)

    for ha in range(4):
        for g in range(3):
            t0 = tile_box[0]
            tile_box[0] += (4 if g == 2 else 3)
            a_group(ha, g, t0)
    assert tile_box[0] == NT2

    if upto <= 4:
        return finish()
    wab_v = wab_d.rearrange("(k p) n -> p k n", p=128)
    mTb = [big[0][:].bitcast(BF16).rearrange("p (j t) -> p j t", t=512),
           big[1][:].bitcast(BF16).rearrange("p (j t) -> p j t", t=512)]
    a2all_t = [a2_t, cst_t[0], qn_t[0], rt_t[0]]

    def mslice(j):
        return mTb[j // 16][:, j % 16, :], big_t[j // 16]

    def o_pair(c, hs, jp):
        wsG = [load_w(wmg_v, (4 * jp) * 128, 256), load_w(wmg_v, (4 * jp + 2) * 128, 256)]
        P.op("gpsimd", lambda e: e.dma_start(out=wabs, in_=wab_v[:, :, jp * 256:(jp + 1) * 256]),
             writes=[wabs_t] + arena_t, dma=True)
        ka = [r * 12 + u for r in range(2) for u in range(8, 12)]
        kb = [r * 12 + u for r in range(2) for u in range(8)]
        for jj in range(2):
            j = 2 * jp + jj
            for (ks, bk) in ((ka, AUX), (kb, SB_)):
                for ii, kc in enumerate(ks):
                    P.op("tensor", lambda e, kc=kc, ii=ii, ks=ks, bk=bk, jj=jj: e.matmul(
                        bank[bk][:], lhsT=wabs[:, kc, jj * 128:(jj + 1) * 128], rhs=ych[:, kc, :],
                        start=(ii == 0), stop=(ii == len(ks) - 1)),
                        reads=[wabs_t, ych_t], writes=[bank_t[bk]])
            bga = mm_tile(wsG[jj], 0, hs)
            bgb = mm_tile(wsG[jj], 1, hs)
            P.op("scalar", lambda e, bga=bga, j=j: e.activation(out=zt[0][:], in_=bank[bga][:], func=AF.Sigmoid,
                                                                bias=bmgT[:, 2 * j:2 * j + 1]),
                 reads=[bank_t[bga], const_t], writes=[zt_t[0]])
            P.op("scalar", lambda e, bgb=bgb, j=j: e.activation(out=sd[0][:], in_=bank[bgb][:], func=AF.Sigmoid,
                                                                bias=bmgT[:, 2 * j + 1:2 * j + 2]),
                 reads=[bank_t[bgb], const_t], writes=[sd_t[0]])
            P.op("vector", lambda e: e.tensor_tensor(out=zt[0][:], in0=bank[AUX][:], in1=zt[0][:], op=ALU.mult),
                 reads=[bank_t[AUX], zt_t[0]], writes=[zt_t[0]])
            P.op("vector", lambda e: e.tensor_tensor(out=sd[0][:], in0=bank[SB_][:], in1=sd[0][:], op=ALU.mult),
                 reads=[bank_t[SB_], sd_t[0]], writes=[sd_t[0]])
            ms_, mt_ = mslice(j)
            P.op("vector", lambda e, ms_=ms_: e.tensor_tensor(out=ms_, in0=zt[0][:], in1=sd[0][:], op=ALU.add),
                 reads=[zt_t[0], sd_t[0]], writes=[mt_])

    def o_out(c, ip):
        wsO = load_w(wo_v, ip * 256, 256)
        for ii in range(2):
            i_ = 2 * ip + ii
            oi = i_ % 2
            P.op("sync", lambda e, oi=oi, i_=i_: e.dma_start(
                out=xo_s[oi], in_=xo_d[c * 512:(c + 1) * 512, i_ * 128:(i_ + 1) * 128].rearrange("(t p) n -> p t n", p=128)),
                writes=[xo_t[oi]] + a2all_t, dma=True)
            for k in range(32):
                ms_, mt_ = mslice(k)
                P.op("tensor", lambda e, k=k, ii=ii, ms_=ms_: e.matmul(
                    bank[OB_][:], lhsT=wslot[wsO][:, k, ii * 128:(ii + 1) * 128], rhs=ms_,
                    start=(k == 0), stop=(k == 31)),
                    reads=[wslot_t[wsO], mt_], writes=[bank_t[OB_]])
            P.op("scalar", lambda e, i_=i_: e.activation(out=zt[0][:], in_=bank[OB_][:], func=AF.Identity,
                                                         scale=gateT[:, i_:i_ + 1]),
                 reads=[bank_t[OB_], mod_t], writes=[zt_t[0]])
            for q in range(4):
                P.op("tensor", lambda e, q=q: e.transpose(out=bank[DB_][:, q * 128:(q + 1) * 128],
                                                          in_=zt[0][:, q * 128:(q + 1) * 128], identity=identF[:]),
                     reads=[zt_t[0], const_t], writes=[bank_t[DB_]])
            P.op("vector", lambda e, oi=oi: e.tensor_tensor(
                out=ob_s[oi], in0=bank[DB_][:].rearrange("p (t n) -> p t n", n=128), in1=xo_s[oi], op=ALU.add),
                reads=[bank_t[DB_], xo_t[oi]], writes=[ob_t[oi]])
            P.op("sync", lambda e, oi=oi, i_=i_: e.dma_start(
                out=out_d[c * 512:(c + 1) * 512, i_ * 128:(i_ + 1) * 128].rearrange("(t p) n -> p t n", p=128), in_=ob_s[oi]),
                reads=[ob_t[oi]], writes=[T("outd")], dma=True)

    ych_pt = [T("ychp%d" % k) for k in range(24)]
    tmp_pt = [T("tmpp%d" % k) for k in range(24)]

    def o_chunk(c):
        hs = 0
        load_h(hTo_d, "o", c, hs=0)
        tmp = hslot[1][:, 0:24, :]
        tmp_t = hslot_t[1]
        for r in range(2):
            for u in range(12):
                kc = r * 12 + u
                P.op("sync", lambda e, r=r, u=u, kc=kc: e.dma_start(
                    out=ych[:, kc, :], in_=ya_p[u][0].ap()[r * 128:(r + 1) * 128, c * 512:(c + 1) * 512]),
                    reads=[ya_t[u][0]], writes=[ych_pt[kc]] + (arena_t + [ych_t] if kc == 0 else []), dma=True)
                P.op("sync", lambda e, r=r, u=u, kc=kc: e.dma_start(
                    out=tmp[:, kc, :], in_=ya_p[u][1].ap()[r * 128:(r + 1) * 128, c * 512:(c + 1) * 512]),
                    reads=[ya_t[u][1]], writes=[tmp_pt[kc]] + ([tmp_t] if kc == 0 else []), dma=True)
        P.op("vector", lambda e: e.tensor_scalar(out=tmp, in0=tmp, scalar1=msk[:, 1:2], scalar2=None, op0=ALU.mult),
             reads=[tmp_t, const_t] + tmp_pt, writes=[tmp_t] + tmp_pt)
        P.op("vector", lambda e: e.scalar_tensor_tensor(out=ych, in0=ych, scalar=msk[:, 0:1], in1=tmp, op0=ALU.mult, op1=ALU.add),
             reads=[ych_t, tmp_t, const_t] + tmp_pt + ych_pt, writes=[ych_t] + ych_pt)
        for jp in range(16):
            o_pair(c, hs, jp)
        for ip in range(16):
            o_out(c, ip)

    for c in range(NCH // 2):
        o_chunk(c)

    return finish()


A_Q0, A_K0, A_V0, A_G0 = 0, 3072, 6144, 9216
B_Q0, B_K0, B_V0, B_G0, B_F0 = 10240, 12288, 14336, 16384, 18432
M_G0 = 18448


def _consts():
    half = 64
    inv = 10000.0 ** (-np.arange(half, dtype=np.float64) / half)
    ang = np.arange(S, dtype=np.float64)[None, :] * np.concatenate([inv, inv])[:, None]
    cos = np.cos(ang)
    sin = np.sin(ang)
    sgn = np.concatenate([-np.ones(half), np.ones(half)])[:, None]
    cs = np.stack([cos, sgn * sin]).astype(np.float32)
    identF = np.eye(128, dtype=np.float32)
    selF = np.zeros((8, 8, 128), np.float32)
    for h in range(8):
        selF[h, h, :] = 1.0
    k = np.arange(128)[:, None]
    q = np.arange(512)[None, :]
    q1 = np.arange(128)[None, :]
    maskB = np.where(q1 >= k, 0.0, NEG).astype(np.float32)
    maskA = np.concatenate([(q1 >= k), (q1 <= k)], axis=1).astype(np.float32)
    return cs, identF, selF.reshape(8, 1024), maskB, maskA


def _tm(v):
    return np.ascontiguousarray(v.reshape(-1, 128).T)


_NC = None
_UPTO = 9
_DBG = False
_RAW = False


def kernel(x, c, norm_g, w_ada, b_ada, w_in, b_in, a_q_norm, a_k_norm, b_q_norm, b_k_norm,
           w_a_out, w_b_out, w_o):
    global _NC
    if _NC is None:
        _NC = build(_UPTO, _DBG)
    nc = _NC
    f = np.float32
    x = np.asarray(x, f); c = np.asarray(c, f)
    w_in0 = np.asarray(w_in, f)[0]; b_in0 = np.asarray(b_in, f)[0]
    cs, identF, selF, maskB, maskA = _consts()
    wada = np.ascontiguousarray(np.asarray(w_ada, f)[0])
    badaT = _tm(np.asarray(b_ada, f)[0])
    gT = _tm(np.asarray(norm_g, f)[0])
    wo = np.ascontiguousarray(np.asarray(w_o, f)[0])
    wa = np.asarray(w_a_out, f)[0]; wb = np.asarray(w_b_out, f)[0]
    wab = np.concatenate([np.concatenate([wb[r * 1024:(r + 1) * 1024], wa[r * 512:(r + 1) * 512]], axis=0) for r in range(2)], axis=0)
    wab = np.ascontiguousarray(wab)
    mgcols = np.concatenate([np.concatenate([M_G0 + j * 128 + np.arange(128), M_G0 + D + j * 128 + np.arange(128)]) for j in range(32)])
    wmg = np.ascontiguousarray(w_in0[:, mgcols])
    bmgT = _tm(b_in0[mgcols])
    per_core_w = []
    for p in range(2):
        cols = []
        fc = np.concatenate([B_F0 + 8 * p + np.arange(8), np.full(120, -1)])
        cols.append(fc)
        for j in range(8):
            h = 8 * p + j
            for base in (B_Q0, B_K0, B_V0, B_G0):
                cols.append(base + h * 128 + np.arange(128))
        for j in range(4):
            h = 4 * p + j
            for g in range(3):
                for base in (A_Q0, A_K0, A_V0):
                    cols.append(base + (g * 8 + h) * 128 + np.arange(128))
                if g == 2:
                    cols.append(A_G0 + h * 128 + np.arange(128))
        cols = np.concatenate(cols)
        assert cols.shape[0] == NT2 * 128
        w2 = w_in0[:, np.maximum(cols, 0)].copy()
        w2[:, cols < 0] = 0.0
        b2 = b_in0[np.maximum(cols, 0)].copy()
        b2[cols < 0] = 0.0
        per_core_w.append((np.ascontiguousarray(w2), _tm(b2)))
    gains = np.stack([np.asarray(b_q_norm, f)[0], np.asarray(b_k_norm, f)[0],
                      np.asarray(a_q_norm, f)[0, 0], np.asarray(a_q_norm, f)[0, 1], np.asarray(a_q_norm, f)[0, 2],
                      np.asarray(a_k_norm, f)[0, 0], np.asarray(a_k_norm, f)[0, 1], np.asarray(a_k_norm, f)[0, 2]], axis=1)
    gains = np.ascontiguousarray(gains.astype(f))
    in_maps = []
    for core in range(8):
        b, p = core // 2, core % 2
        w2, b2T = per_core_w[p]
        m = np.zeros((128, 2), f)
        m[:, p] = 1.0
        in_maps.append({
            "x": np.ascontiguousarray(x[b]), "xo": np.ascontiguousarray(x[b, p * 2048:(p + 1) * 2048]),
            "cT": _tm(c[b]), "wada": wada, "badaT": badaT, "gT": gT, "w2": w2, "b2T": b2T,
            "wmg": wmg, "bmgT": bmgT, "wab": wab, "wo": wo, "gains": gains, "cs": cs, "msk": m,
            "identF": identF, "selF": selF, "maskT": maskB, "maskA": maskA,
        })
    if _RAW:
        return run_bass_kernel_spmd(nc, in_maps, core_ids=list(range(8)))
    res = run_bass_kernel_spmd(nc, in_maps, core_ids=list(range(8)))
    out = np.empty((4, S, D), f)
    for core in range(8):
        b, p = core // 2, core % 2
        out[b, p * 2048:(p + 1) * 2048] = res.results[core]["out"]
    return out
```

```python
import contextlib
import numpy as np
import concourse.bass as bass
import concourse.mybir as mybir
from concourse.bass_utils import run_bass_kernel_spmd

F32 = mybir.dt.float32
BF16 = mybir.dt.bfloat16
ALU = mybir.AluOpType
AF = mybir.ActivationFunctionType

D = 4096
S = 4096
NCH = 8
EPS = 1e-6
SCALE = 128 ** -0.5
DILS = (1, 4, 16)
PERIOD = (1, 1, 4)
NT2 = 1 + 8 * 4 + 4 * 10
NEG = -30000.0


class T:
    __slots__ = ("name", "w", "r")

    def __init__(self, name):
        self.name = name
        self.w = None
        self.r = []


class I:
    __slots__ = ("eng", "fn", "deps", "signal", "dma", "sem", "val", "pos", "cc")


class Prog:
    ENGS = ("tensor", "vector", "scalar", "gpsimd", "sync")

    def __init__(self):
        self.streams = {e: [] for e in self.ENGS}

    def op(self, eng, fn, reads=(), writes=(), dma=False, cc=False):
        i = I()
        i.eng, i.fn, i.dma, i.signal, i.cc = eng, fn, dma, False, cc
        i.sem = i.val = None
        i.pos = len(self.streams[eng])
        deps = []
        for t in reads:
            if t.w is not None:
                deps.append(t.w)
        for t in writes:
            if t.w is not None:
                deps.append(t.w)
            deps.extend(t.r)
        best = {}
        out = []
        for d in deps:
            if d is i:
                continue
            if d.dma or d.cc:
                out.append(d)
            else:
                if d.eng == "tensor" and eng == "tensor":
                    continue
                b = best.get(d.eng)
                if b is None or d.pos > b.pos:
                    best[d.eng] = d
        out.extend(best.values())
        i.deps = out
        for d in out:
            d.signal = True
        for t in reads:
            if not (dma or cc):
                t.r = [x for x in t.r if x.dma or x.cc or x.eng != eng]
            t.r.append(i)
        for t in writes:
            t.w = i
            t.r = []
        self.streams[eng].append(i)
        return i

    def emit(self, nc, es):
        NPOOL = 12
        csem = {e: es.enter_context(nc.semaphore("c_" + e)) for e in self.ENGS}
        dsem = {e: [es.enter_context(nc.semaphore("d_%s%d" % (e, k))) for k in range(NPOOL)]
                for e in ("sync", "gpsimd")}
        ccsem = es.enter_context(nc.semaphore("ccs"))
        for e in self.ENGS:
            cnt = 0
            dcnt = [0] * NPOOL
            rr = 0
            ccn = 0
            for i in self.streams[e]:
                if i.cc:
                    ccn += 1
                    i.sem, i.val = ccsem, ccn
                    i.signal = True
                elif i.dma:
                    k = rr % NPOOL
                    rr += 1
                    dcnt[k] += 16
                    i.sem, i.val = dsem[e][k], dcnt[k]
                    i.signal = True
                elif i.signal:
                    cnt += 1
                    i.sem, i.val = csem[e], cnt
        block = es.enter_context(nc.Block())

        def run(e):
            def body(eng):
                waited = {}
                for i in self.streams[e]:
                    need = {}
                    for d in i.deps:
                        k = id(d.sem)
                        if waited.get(k, 0) >= d.val:
                            continue
                        if k not in need or need[k][1] < d.val:
                            need[k] = (d.sem, d.val)
                    for k, (s, v) in need.items():
                        eng.wait_ge(s, v)
                        waited[k] = v
                    ins = i.fn(eng)
                    if i.cc:
                        ins.then_inc(i.sem)
                    elif i.dma:
                        ins.then_inc(i.sem, 16)
                    elif i.signal:
                        ins.then_inc(i.sem, 1)
                if e in dsem:
                    last = {}
                    for i in self.streams[e]:
                        if i.dma or i.cc:
                            last[id(i.sem)] = (i.sem, i.val)
                    for s, v in last.values():
                        eng.wait_ge(s, v)
            return body

        block.tensor(run("tensor"))
        block.vector(run("vector"))
        block.scalar(run("scalar"))
        block.gpsimd(run("gpsimd"))
        block.sync(run("sync"))


def build(upto=9, dbg=False):
    nc = bass.Bass("TRN2", target_bir_lowering=False)
    es = contextlib.ExitStack()
    P = Prog()

    def din(name, shape, dt=F32):
        return nc.dram_tensor(name, list(shape), dt, kind="ExternalInput").ap()

    x_d = din("x", [S, D])
    xo_d = din("xo", [S // 2, D])
    cT_d = din("cT", [128, 32])
    wada_d = din("wada", [D, 3 * D])
    badaT_d = din("badaT", [128, 96])
    gT_d = din("gT", [128, 32])
    w2_d = din("w2", [D, NT2 * 128])
    b2T_d = din("b2T", [128, NT2])
    wmg_d = din("wmg", [D, 2 * D])
    bmgT_d = din("bmgT", [128, 64])
    wab_d = din("wab", [24 * 128, D])
    wo_d = din("wo", [D, D])
    gains_d = din("gains", [128, 8])
    cs_d = din("cs", [2, 128, S])
    msk_d = din("msk", [128, 2])
    identF_d = din("identF", [128, 128])
    selF_d = din("selF", [8, 8 * 128])
    maskT_d = din("maskT", [128, 128])
    maskA_d = din("maskA", [128, 256])
    out_d = nc.dram_tensor("out", [S // 2, D], F32, kind="ExternalOutput").ap()

    if dbg:
        dbg_mod = nc.dram_tensor("dbg_mod", [128, 96], F32, kind="ExternalOutput").ap()
        dbg_F = nc.dram_tensor("dbg_F", [8, S], F32, kind="ExternalOutput").ap()
        dbg_Ftok = nc.dram_tensor("dbg_Ftok", [128, 256], F32, kind="ExternalOutput").ap()
        dbg_ys = nc.dram_tensor("dbg_ys", [12 * 128, S], BF16, kind="ExternalOutput").ap()
        dbg_h = nc.dram_tensor("dbg_h", [128, 32 * 512], BF16, kind="ExternalOutput").ap()
    hT_d = nc.dram_tensor("hT_d", [NCH, 128, 32 * 512], BF16)
    hTo_d = nc.dram_tensor("hTo_d", [NCH // 2, 128, 32 * 512], BF16)
    ys_p = [[nc.dram_tensor("ys_%d_%d" % (u, hf), [128, S // 2], BF16) for hf in range(2)] for u in range(12)]
    ya_p = [[nc.dram_tensor("ya_%d_%d" % (u, hf), [256, S // 2], BF16) for hf in range(2)] for u in range(12)]
    ya_t = [[T("ya_%d_%d" % (u, hf)) for hf in range(2)] for u in range(12)]

    def ys_dst(u, c):
        return ys_p[u][c // 4].ap()[:, (c % 4) * 512:(c % 4 + 1) * 512]

    def exchange(u, hf):
        P.op("gpsimd", lambda e: e.collective_compute("AllGather", ALU.bypass, replica_groups=[[0, 1], [2, 3], [4, 5], [6, 7]],
                                                      ins=[ys_p[u][hf].ap().opt()], outs=[ya_p[u][hf].ap().opt()]),
             reads=[ys_t[(u, c)] for c in range(4 * hf, 4 * hf + 4)], writes=[ya_t[u][hf]], cc=True)

    def sb(name, shape, dt):
        return es.enter_context(nc.sbuf_tensor(name + "_s", list(shape), dt))

    def pst(name, shape, dt):
        return es.enter_context(nc.psum_tensor(name, list(shape), dt))

    wslot = [sb("wslot%d" % k, [128, 32, 256], BF16) for k in range(3)]
    wslot_t = [T("wslot%d" % k) for k in range(3)]
    hslot = [sb("hslot%d" % k, [128, 32, 512], BF16) for k in range(2)]
    hslot_t = [T("hslot%d" % k) for k in range(2)]
    big = [sb("big%d" % k, [128, 4096], F32) for k in range(2)]
    big_t = [T("big%d" % k) for k in range(2)]
    arena = sb("arena", [128, 18 * 1024], BF16)
    arena2 = sb("arena2", [128, 5 * 1024], BF16)
    off = [0]

    def carve(nelem_bf16, ar=None, lim=18 * 1024):
        ar = arena if ar is None else ar
        a = ar[:, off[0]:off[0] + nelem_bf16]
        off[0] += nelem_bf16
        assert off[0] <= lim, off[0]
        return a

    def c2(nelem_bf16):
        return carve(nelem_bf16, arena2, 5 * 1024)

    kT = carve(4096)
    Vt = carve(4096).rearrange("p (t d) -> p t d", d=128)
    gA = carve(2048)
    qp = carve(2048)
    vTp = carve(2048)
    kT_t, Vt_t, gA_t, qp_t, vTp_t = T("kT"), T("Vt"), T("gA"), T("qp"), T("vTp")
    arena_t = [kT_t, Vt_t, gA_t, qp_t, vTp_t]
    off[0] = 0
    ych = carve(24 * 512).rearrange("p (k t) -> p k t", t=512)
    wabs = carve(24 * 256).rearrange("p (k n) -> p k n", n=256)
    ych_t, wabs_t = T("ych"), T("wabs")
    mT_t = T("mT")
    a2_t = T("arena2")
    off[0] = 0
    Fb = [c2(1024).bitcast(F32)]
    tS = [c2(1024).bitcast(F32), c2(1024).bitcast(F32)]
    selF = c2(2048).bitcast(F32)[0:8, :]
    off[0] = 0
    cst = [c2(2048).bitcast(F32).rearrange("p (a t) -> p a t", a=2)]
    qn = [c2(1024).bitcast(F32), None]
    rt = [c2(1024).bitcast(F32), None]
    off[0] = 0
    xo_s = [c2(1024).bitcast(F32).rearrange("p (t n) -> p t n", n=128) for k in range(2)]
    ob_s = [c2(1024).bitcast(F32).rearrange("p (t n) -> p t n", n=128) for k in range(2)]
    Fb_t = [T('Fb0')]
    tS_t = [T('tS0'), T('tS1')]
    cst_t = [T("cst")]
    qn_t = [T("qn0"), None]
    rt_t = [T("rt0"), None]
    xo_t = [T("xo0"), T("xo1")]
    ob_t = [T("ob0"), T("ob1")]

    identF = sb("identF", [128, 128], F32)
    identB = sb("identB", [128, 128], BF16)
    onesB = sb("onesB", [128, 128], BF16)
    maskT = sb("maskT", [128, 128], F32)
    maskA = sb("maskA", [128, 256], BF16)
    cT = sb("cT", [128, 32], F32)
    scB = sb("scB", [128, 32], BF16)
    modT = sb("modT", [128, 96], F32)
    badaT = sb("badaT", [128, 96], F32)
    gT = sb("gT", [128, 32], F32)
    sc1 = sb("sc1", [128, 32], F32)
    b2T = sb("b2T", [128, NT2], F32)
    bmgT = sb("bmgT", [128, 64], F32)
    gains = sb("gains", [128, 8], F32)
    msk = sb("msk", [128, 2], F32)
    Ftok = sb("Ftok", [128, 32 * 8], F32)
    zt = [sb("zt0", [128, 512], F32)]
    zt_t = [T("zt0")]
    sqb = [sb("sqb0", [128, 512], BF16)]
    sqb_t = [T("sqb0")]
    sd = [sb("sd0", [128, 512], F32)]
    sd_t = [T("sd0")]
    qc = [sb("qc%d" % k, [128, 512], BF16) for k in range(1)]
    qc_t = [T("qc%d" % k) for k in range(1)]
    gc = [sb("gc%d" % k, [128, 512], BF16) for k in range(1)]
    gc_t = [T("gc%d" % k) for k in range(1)]
    vTc = [sb("vTc%d" % k, [128, 512], BF16) for k in range(1)]
    vTc_t = [T("vTc%d" % k) for k in range(1)]
    PT = [sb("PT%d" % k, [128, 512], BF16) for k in range(3)]
    PT_t = [T("PT%d" % k) for k in range(3)]
    PTs_t = [T("PTs%d" % k) for k in range(12)]
    Ss_t = [T("Ss%d" % k) for k in range(8)]
    Os_t = [T("Os%d" % k) for k in range(4)]
    Ds_t = [T("Ds%d" % k) for k in range(4)]
    ybc = [sb("ybc%d" % k, [128, 512], BF16) for k in range(1)]
    ybc_t = [T("ybc%d" % k) for k in range(1)]
    ss = sb("ss", [128, 4], F32)
    ss_t = T("ss")
    const_t = T("const")
    mod_t = T("mod")
    Ftok_t = T("Ftok")

    bank = [pst("bank%d" % k, [128, 512], F32) for k in range(7)]
    bank_t = [T("bank%d" % k) for k in range(7)]
    pbB = pst("pbB", [128, 1024], BF16)
    pbB_t = T("pbB")
    IN = [0, 1]
    SB2_ = 2
    AUX, SB_, OB_, DB_ = 3, 4, 5, 6

    rr = {"in": 0, "pt": 0, "w": 0, "h": 0, "ss": 0, "od": 0, "pts": 0}
    ys_t = {}

    def finish():
        if dbg:
            P.op("sync", lambda e: e.dma_start(out=dbg_mod, in_=modT[:]), reads=[mod_t], writes=[T("d1")], dma=True)
            if upto >= 1:
                P.op("sync", lambda e: e.dma_start(out=dbg_h, in_=hT_d[0]), reads=[hd_t[("a", 0)]], writes=[T("d2")], dma=True)
            if upto >= 3:
                for u in range(12):
                    for hf in range(2):
                        P.op("sync", lambda e, u=u, hf=hf: e.dma_start(out=dbg_ys[u * 128:(u + 1) * 128, hf * 2048:(hf + 1) * 2048],
                                                                       in_=ys_p[u][hf].ap()),
                             reads=[ys_t[(u, c)] for c in range(4 * hf, 4 * hf + 4) if (u, c) in ys_t], writes=[T("d3")], dma=True)
        P.emit(nc, es)
        es.close()
        return nc

    def nxt(k, n):
        v = rr[k] % n
        rr[k] += 1
        return v

    def ld(dst, src, t, eng="sync"):
        P.op(eng, lambda e: e.dma_start(out=dst, in_=src), writes=[t], dma=True)

    for dst, src in ((identF[:], identF_d), (maskT[:], maskT_d), (zt[0][:, 0:256], maskA_d),
                     (cT[:], cT_d), (badaT[:], badaT_d), (gT[:], gT_d),
                     (b2T[:], b2T_d), (bmgT[:], bmgT_d), (gains[:], gains_d), (msk[:], msk_d)):
        ld(dst, src, const_t)
    P.op("vector", lambda e: e.tensor_copy(out=identB[:], in_=identF[:]), reads=[const_t], writes=[const_t])
    P.op("vector", lambda e: e.memset(onesB[:], 1.0), writes=[const_t])
    P.op("vector", lambda e: e.tensor_copy(out=maskA[:], in_=zt[0][:, 0:256]), reads=[const_t], writes=[const_t, zt_t[0]])
    P.op("scalar", lambda e: e.activation(out=scB[:], in_=cT[:], func=AF.Silu), reads=[const_t], writes=[const_t])

    wada_v = wada_d.rearrange("(k p) n -> p k n", p=128)

    def ph0(cg):
        ws = nxt("w", 3)
        P.op("gpsimd", lambda e: e.dma_start(out=wslot[ws][:], in_=wada_v[:, :, cg * 256:(cg + 1) * 256]),
             writes=[wslot_t[ws]], dma=True)
        for j in range(2):
            ct = cg * 2 + j
            for k in range(32):
                P.op("tensor", lambda e, j=j, k=k, ct=ct: e.matmul(
                    bank[AUX][:, ct:ct + 1], lhsT=wslot[ws][:, k, j * 128:(j + 1) * 128], rhs=scB[:, k:k + 1],
                    start=(k == 0), stop=(k == 31)),
                    reads=[wslot_t[ws], const_t], writes=[bank_t[AUX]])
    for cg in range(48):
        ph0(cg)
    P.op("vector", lambda e: e.tensor_tensor(out=modT[:], in0=bank[AUX][:, 0:96], in1=badaT[:], op=ALU.add),
         reads=[bank_t[AUX], const_t], writes=[mod_t])
    P.op("vector", lambda e: e.scalar_tensor_tensor(out=sc1[:], in0=modT[:, 32:64], scalar=1.0, in1=gT[:],
                                                    op0=ALU.add, op1=ALU.mult),
         reads=[mod_t, const_t], writes=[mod_t])
    shiftT = modT[:, 0:32]
    gateT = modT[:, 64:96]

    if upto <= 0:
        return finish()
    junk = arena[:, 0:4096]
    junk_t = kT_t
    hd_t = {}
    for ch in range(NCH):
        hd_t[("a", ch)] = T("hTd%d" % ch)
    for ch in range(NCH // 2):
        hd_t[("o", ch)] = T("hTod%d" % ch)

    def h_tile(src, dst, key, tt):
        xb = tt % 2
        xt = big[xb]
        tl = tt % 4
        ch = tt // 4
        hs = ch % 2
        P.op("sync", lambda e: e.dma_start(out=xt[:], in_=src[tt * 128:(tt + 1) * 128, :]),
             writes=[big_t[xb]], dma=True)
        P.op("scalar", lambda e: e.activation(out=junk, in_=xt[:], func=AF.Square, accum_out=ss[:, 0:1]),
             reads=[big_t[xb]], writes=[junk_t, ss_t])
        P.op("scalar", lambda e: e.activation(out=ss[:, 1:2], in_=ss[:, 0:1], func=AF.Sqrt, scale=1.0 / D, bias=EPS),
             reads=[ss_t], writes=[ss_t])
        P.op("vector", lambda e: e.reciprocal(out=ss[:, 2:3], in_=ss[:, 1:2]), reads=[ss_t], writes=[ss_t])
        P.op("vector", lambda e: e.tensor_scalar(out=xt[:], in0=xt[:], scalar1=ss[:, 2:3], scalar2=None, op0=ALU.mult),
             reads=[ss_t, big_t[xb]], writes=[big_t[xb]])
        for kg in range(8):
            bk = IN[nxt("in", 2)]
            for q in range(4):
                kc = kg * 4 + q
                P.op("tensor", lambda e, kc=kc, q=q, bk=bk: e.transpose(
                    out=bank[bk][:, q * 128:(q + 1) * 128], in_=xt[:, kc * 128:(kc + 1) * 128], identity=identF[:]),
                    reads=[big_t[xb], const_t], writes=[bank_t[bk]])
            for q in range(4):
                kc = kg * 4 + q
                if q % 2 == 0:
                    P.op("vector", lambda e, kc=kc, q=q, bk=bk: e.tensor_scalar(
                        out=hslot[hs][:, kc, tl * 128:(tl + 1) * 128], in0=bank[bk][:, q * 128:(q + 1) * 128],
                        scalar1=sc1[:, kc:kc + 1], scalar2=shiftT[:, kc:kc + 1], op0=ALU.mult, op1=ALU.add),
                        reads=[bank_t[bk], mod_t], writes=[hslot_t[hs]])
                else:
                    P.op("scalar", lambda e, kc=kc, q=q, bk=bk: e.activation(
                        out=hslot[hs][:, kc, tl * 128:(tl + 1) * 128], in_=bank[bk][:, q * 128:(q + 1) * 128],
                        func=AF.Identity, scale=sc1[:, kc:kc + 1], bias=shiftT[:, kc:kc + 1]),
                        reads=[bank_t[bk], mod_t], writes=[hslot_t[hs]])
        if tl == 3:
            P.op("sync", lambda e: e.dma_start(out=dst[ch], in_=hslot[hs][:].rearrange("p k t -> p (k t)")),
                 reads=[hslot_t[hs]], writes=[hd_t[(key, ch)]], dma=True)

    for tt in range(32):
        h_tile(x_d, hT_d, "a", tt)
    for tt in range(16):
        h_tile(xo_d, hTo_d, "o", tt)

    if upto <= 1:
        return finish()
    w2_v = w2_d.rearrange("(k p) n -> p k n", p=128)
    wmg_v = wmg_d.rearrange("(k p) n -> p k n", p=128)
    wo_v = wo_d.rearrange("(k p) n -> p k n", p=128)

    def load_w(view, col0, ncols):
        ws = nxt("w", 3)
        P.op("gpsimd", lambda e: e.dma_start(out=wslot[ws][:, :, 0:ncols], in_=view[:, :, col0:col0 + ncols]),
             writes=[wslot_t[ws]], dma=True)
        return ws

    def load_h(src, key, ch, hs=None):
        if hs is None:
            hs = nxt("h", 2)
        P.op("sync", lambda e: e.dma_start(out=hslot[hs][:].rearrange("p k t -> p (k t)"), in_=src[ch]),
             reads=[hd_t[(key, ch)]], writes=[hslot_t[hs]], dma=True)
        return hs

    hq = {"slot": None, "c": None}

    def get_h(c):
        if hq["c"] == c:
            hs = hq["slot"]
        else:
            hs = load_h(hT_d, "a", c)
        n_ = (c + 1) % NCH
        hq["slot"] = load_h(hT_d, "a", n_)
        hq["c"] = n_
        return hs

    wq = {}

    def prefetch_w(col0, ncols):
        wq[col0] = load_w(w2_v, col0, ncols)

    def get_w(col0, ncols):
        if col0 in wq:
            return wq.pop(col0)
        return load_w(w2_v, col0, ncols)

    def mm_tile(ws, wj, hs):
        bk = IN[nxt("in", 2)]
        for k in range(32):
            P.op("tensor", lambda e, k=k: e.matmul(bank[bk][:], lhsT=wslot[ws][:, k, wj * 128:(wj + 1) * 128],
                                                   rhs=hslot[hs][:, k, :], start=(k == 0), stop=(k == 31)),
                 reads=[wslot_t[ws], hslot_t[hs]], writes=[bank_t[bk]])
        return bk

    def qk_norm(bk, bcol, gcol, out_ap, out_t):
        z = 0
        P.op("scalar", lambda e: e.activation(out=zt[z][:], in_=bank[bk][:], func=AF.Identity, bias=b2T[:, bcol:bcol + 1]),
             reads=[bank_t[bk], const_t], writes=[zt_t[z]])
        P.op("scalar", lambda e: e.activation(out=sqb[z][:], in_=bank[bk][:], func=AF.Square, bias=b2T[:, bcol:bcol + 1]),
             reads=[bank_t[bk], const_t], writes=[sqb_t[z]])
        P.op("tensor", lambda e: e.matmul(bank[AUX][:], lhsT=onesB[:], rhs=sqb[z][:], start=True, stop=True),
             reads=[sqb_t[z], const_t], writes=[bank_t[AUX]])
        P.op("scalar", lambda e: e.activation(out=sd[z][:], in_=bank[AUX][:], func=AF.Sqrt, scale=1.0 / 128, bias=EPS),
             reads=[bank_t[AUX]], writes=[sd_t[z]])
        P.op("vector", lambda e: e.reciprocal(out=sd[z][:], in_=sd[z][:]), reads=[sd_t[z]], writes=[sd_t[z]])
        P.op("vector", lambda e: e.scalar_tensor_tensor(out=out_ap, in0=zt[z][:], scalar=gains[:, gcol:gcol + 1],
                                                        in1=sd[z][:], op0=ALU.mult, op1=ALU.mult),
             reads=[zt_t[z], sd_t[z], const_t], writes=[out_t])

    FT = big[0]
    lf = big[1]

    def f_chunk(ws, c):
        hs = get_h(c)
        if c == 4:
            prefetch_w(128, 256)
        bk = mm_tile(ws, 0, hs)
        cs_ = slice(c * 512, (c + 1) * 512)
        P.op("scalar", lambda e: e.activation(out=lf[0:8, cs_], in_=bank[bk][0:8, :], func=AF.Identity, bias=b2T[0:8, 0:1]),
             reads=[bank_t[bk], const_t], writes=[big_t[1]])
        P.op("scalar", lambda e: e.activation(out=FT[0:8, cs_], in_=lf[0:8, cs_], func=AF.Abs),
             reads=[big_t[1]], writes=[big_t[0]])
        P.op("scalar", lambda e: e.activation(out=FT[0:8, cs_], in_=FT[0:8, cs_], func=AF.Exp, scale=-1.0),
             reads=[big_t[0]], writes=[big_t[0]])
        P.op("scalar", lambda e: e.activation(out=FT[0:8, cs_], in_=FT[0:8, cs_], func=AF.Ln, bias=1.0),
             reads=[big_t[0]], writes=[big_t[0]])
        P.op("vector", lambda e: e.tensor_single_scalar(out=lf[0:8, cs_], in_=lf[0:8, cs_], scalar=0.0, op=ALU.min),
             reads=[big_t[1]], writes=[big_t[1]])
        P.op("vector", lambda e: e.tensor_tensor(out=lf[0:8, cs_], in0=lf[0:8, cs_], in1=FT[0:8, cs_], op=ALU.subtract),
             reads=[big_t[1], big_t[0]], writes=[big_t[1]])

    ws0 = load_w(w2_v, 0, 128)
    for c in range(NCH):
        f_chunk(ws0, c)
    P.op("vector", lambda e: e.tensor_tensor_scan(out=FT[0:8, :], data0=lf[0:8, :], data1=lf[0:8, :], initial=0.0,
                                                  op0=ALU.add, op1=ALU.bypass),
         reads=[big_t[1]], writes=[big_t[0]])
    for tt in range(32):
        P.op("tensor", lambda e, tt=tt: e.transpose(out=bank[AUX][:, 0:8], in_=FT[0:8, tt * 128:(tt + 1) * 128],
                                                    identity=identF[0:8, 0:8]),
             reads=[big_t[0], const_t], writes=[bank_t[AUX]])
        P.op("scalar", lambda e, tt=tt: e.mul(out=Ftok[:, tt * 8:(tt + 1) * 8], in_=bank[AUX][:, 0:8], mul=-1.0),
             reads=[bank_t[AUX]], writes=[Ftok_t])
    P.op("sync", lambda e: e.dma_start(out=selF, in_=selF_d), writes=[a2_t], dma=True)
    if dbg:
        P.op("sync", lambda e: e.dma_start(out=dbg_F, in_=FT[0:8, :]), reads=[big_t[0]], writes=[T("d4")], dma=True)
        P.op("sync", lambda e: e.dma_start(out=dbg_Ftok, in_=Ftok[:]), reads=[Ftok_t], writes=[T("d5")], dma=True)
    if upto <= 2:
        return finish()
    tile_box = [1]

    def b_chunk(hb, t0, wsA, wsB, c):
        hs = get_h(c)
        if c == 4:
            prefetch_w((t0 + 4) * 128, 256)
        cs_ = slice(c * 512, (c + 1) * 512)
        bk = mm_tile(wsA, 0, hs)
        qk_norm(bk, t0, 0, qc[0][:], qc_t[0])
        bk = mm_tile(wsA, 1, hs)
        qk_norm(bk, t0 + 1, 1, kT[:, cs_], kT_t)
        bkv = mm_tile(wsB, 0, hs)
        P.op("scalar", lambda e: e.activation(out=vTc[0][:], in_=bank[bkv][:], func=AF.Identity, bias=b2T[:, t0 + 2:t0 + 3]),
             reads=[bank_t[bkv], const_t], writes=[vTc_t[0]])
        for q in range(4):
            P.op("tensor", lambda e, q=q: e.transpose(out=pbB[:, q * 128:(q + 1) * 128],
                                                      in_=vTc[0][:, q * 128:(q + 1) * 128], identity=identB[:]),
                 reads=[vTc_t[0], const_t], writes=[pbB_t])
        P.op("vector", lambda e: e.tensor_copy(out=Vt[:, 4 * c:4 * c + 4, :],
                                               in_=pbB[:, 0:512].rearrange("p (t d) -> p t d", d=128)),
             reads=[pbB_t], writes=[Vt_t])
        bkg = mm_tile(wsB, 1, hs)
        P.op("scalar", lambda e: e.activation(out=gc[0][:], in_=bank[bkg][:], func=AF.Silu, bias=b2T[:, t0 + 3:t0 + 4]),
             reads=[bank_t[bkg], const_t], writes=[gc_t[0]])
        P.op("tensor", lambda e: e.matmul(bank[AUX][:], lhsT=selF[0:8, hb * 128:(hb + 1) * 128], rhs=FT[0:8, cs_],
                                          start=True, stop=True),
             reads=[big_t[0], a2_t], writes=[bank_t[AUX]])
        P.op("vector", lambda e: e.tensor_copy(out=Fb[0], in_=bank[AUX][:]), reads=[bank_t[AUX]], writes=[Fb_t[0]])
        nkt = 4 * c + 4
        pend = None

        def pv(kt, pi, q0):
            P.op("tensor", lambda e: e.matmul(bank[OB_][:, q0:512], lhsT=Vt[:, kt, :], rhs=PT[pi][:, q0:512],
                                              start=(kt == 0), stop=(kt == nkt - 1)),
                 reads=[Vt_t, PT_t[pi]], writes=[bank_t[OB_]])
            P.op("tensor", lambda e: e.matmul(bank[DB_][:, q0:512], lhsT=onesB[:], rhs=PT[pi][:, q0:512],
                                              start=(kt == 0), stop=(kt == nkt - 1)),
                 reads=[const_t, PT_t[pi]], writes=[bank_t[DB_]])

        for kt in range(nkt):
            j = kt - 4 * c
            q0 = 128 * j if j > 0 else 0
            sbk = (SB_, SB2_)[kt % 2]
            ti = kt % 2
            P.op("tensor", lambda e, kt=kt, q0=q0, sbk=sbk: e.matmul(bank[sbk][:, q0:512], lhsT=kT[:, kt * 128:(kt + 1) * 128],
                                                                    rhs=qc[0][:, q0:512], start=True, stop=True),
                 reads=[kT_t, qc_t[0]], writes=[bank_t[sbk]])
            P.op("vector", lambda e, q0=q0, sbk=sbk, ti=ti: e.scalar_tensor_tensor(
                out=tS[ti][:, q0:512], in0=bank[sbk][:, q0:512], scalar=SCALE, in1=Fb[0][:, q0:512], op0=ALU.mult, op1=ALU.add),
                reads=[bank_t[sbk], Fb_t[0]], writes=[tS_t[ti]])
            if j >= 0:
                P.op("vector", lambda e, q0=q0, ti=ti: e.tensor_tensor(out=tS[ti][:, q0:q0 + 128], in0=tS[ti][:, q0:q0 + 128],
                                                                       in1=maskT[:], op=ALU.add),
                     reads=[tS_t[ti], const_t], writes=[tS_t[ti]])
            pi = nxt("pt", 3)
            P.op("scalar", lambda e, pi=pi, kt=kt, q0=q0, ti=ti: e.activation(
                out=PT[pi][:, q0:512], in_=tS[ti][:, q0:512], func=AF.Exp, bias=Ftok[:, kt * 8 + hb:kt * 8 + hb + 1]),
                reads=[tS_t[ti], Ftok_t], writes=[PT_t[pi]])
            if pend is not None:
                pv(*pend)
            pend = (kt, pi, q0)
        pv(*pend)
        P.op("vector", lambda e: e.reciprocal(out=zt[0][:], in_=bank[DB_][:]), reads=[bank_t[DB_]], writes=[zt_t[0]])
        P.op("vector", lambda e: e.tensor_tensor(out=zt[0][:], in0=bank[OB_][:], in1=zt[0][:], op=ALU.mult),
             reads=[bank_t[OB_], zt_t[0]], writes=[zt_t[0]])
        P.op("vector", lambda e: e.tensor_tensor(out=ybc[0][:], in0=zt[0][:], in1=gc[0][:], op=ALU.mult),
             reads=[zt_t[0], gc_t[0]], writes=[ybc_t[0]])
        yt = T("ys")
        ys_t[(hb, c)] = yt
        P.op("sync", lambda e: e.dma_start(out=ys_dst(hb, c), in_=ybc[0][:]),
             reads=[ybc_t[0]], writes=[yt], dma=True)
        if c % 4 == 3:
            exchange(hb, c // 4)

    for hb in range(8):
        t0 = tile_box[0]
        tile_box[0] += 4
        wsA = get_w(t0 * 128, 256)
        wsB = get_w((t0 + 2) * 128, 256)
        for c in range(NCH):
            b_chunk(hb, t0, wsA, wsB, c)

    if upto <= 3:
        return finish()
    num, den = big[0], big[1]

    def rope(dil, src_ap, src_t, dst_ap, dst_t):
        P.op("scalar", lambda e: e.copy(out=rt[0][0:64, :], in_=src_ap[64:128, :]), reads=[src_t], writes=[rt_t[0]])
        P.op("scalar", lambda e: e.copy(out=rt[0][64:128, :], in_=src_ap[0:64, :]), reads=[src_t], writes=[rt_t[0]])
        P.op("vector", lambda e: e.tensor_tensor(out=rt[0], in0=rt[0], in1=cst[0][:, 1, :], op=ALU.mult),
             reads=[rt_t[0], cst_t[0]], writes=[rt_t[0]])
        P.op("vector", lambda e: e.tensor_tensor(out=src_ap, in0=src_ap, in1=cst[0][:, 0, :], op=ALU.mult),
             reads=[src_t, cst_t[0]], writes=[src_t])
        P.op("vector", lambda e: e.tensor_tensor(
            out=dst_ap, in0=src_ap.rearrange("p (m r) -> p m r", r=dil),
            in1=rt[0].rearrange("p (m r) -> p m r", r=dil), op=ALU.add),
            reads=[src_t, rt_t[0]], writes=[dst_t])

    pendA = [None]

    def flushA():
        if pendA[0] is not None:
            pendA[0]()
            pendA[0] = None

    ablk = [0]

    def a_block(g, dil, nblk, kcm, qpm, r, nbl, n):
        ms = [m for m in (n - 1, n) if m >= 0]
        bi = ablk[0]
        ablk[0] += 1
        sbk = (SB_, SB2_)[bi % 2]
        odb = (OB_, DB_)[bi % 2]
        pi = bi % 3
        nm = len(ms)
        for mi, m in enumerate(ms):
            P.op("tensor", lambda e, m=m, mi=mi: e.matmul(
                bank[sbk][:, mi * 128:(mi + 1) * 128], lhsT=kcm[:, r, m * 128:(m + 1) * 128],
                rhs=qpm[:, r, nbl * 128:(nbl + 1) * 128], start=True, stop=True),
                reads=[kT_t, qp_t], writes=[bank_t[sbk]])
        P.op("scalar", lambda e: e.activation(out=PT[pi][:, 0:nm * 128], in_=bank[sbk][:, 0:nm * 128], func=AF.Exp, scale=SCALE),
             reads=[bank_t[sbk]], writes=[PT_t[pi]])
        mo = 0 if nm == 2 else 128
        P.op("vector", lambda e: e.tensor_tensor(out=PT[pi][:, 0:nm * 128], in0=PT[pi][:, 0:nm * 128],
                                                 in1=maskA[:, mo:mo + nm * 128], op=ALU.mult),
             reads=[PT_t[pi], const_t], writes=[PT_t[pi]])

        def second():
            for mi, m in enumerate(ms):
                P.op("tensor", lambda e, m=m, mi=mi: e.matmul(
                    bank[odb][:, 0:128], lhsT=Vt[:, r * nblk + m, :], rhs=PT[pi][:, mi * 128:(mi + 1) * 128],
                    start=(mi == 0), stop=(mi == nm - 1)),
                    reads=[Vt_t, PT_t[pi]], writes=[bank_t[odb]])
            for mi, m in enumerate(ms):
                P.op("tensor", lambda e, mi=mi: e.matmul(
                    bank[odb][:, 128:256], lhsT=onesB[:], rhs=PT[pi][:, mi * 128:(mi + 1) * 128],
                    start=(mi == 0), stop=(mi == nm - 1)),
                    reads=[const_t, PT_t[pi]], writes=[bank_t[odb]])
            numv = num[:].rearrange("p (l r) -> p r l", r=dil)[:, r, n * 128:(n + 1) * 128]
            denv = den[:].rearrange("p (l r) -> p r l", r=dil)[:, r, n * 128:(n + 1) * 128]
            if g == 0:
                P.op("vector", lambda e: e.tensor_copy(out=numv, in_=bank[odb][:, 0:128]),
                     reads=[bank_t[odb]], writes=[big_t[0]])
                P.op("vector", lambda e: e.tensor_copy(out=denv, in_=bank[odb][:, 128:256]),
                     reads=[bank_t[odb]], writes=[big_t[1]])
            else:
                P.op("vector", lambda e: e.tensor_tensor(out=numv, in0=bank[odb][:, 0:128], in1=numv, op=ALU.add),
                     reads=[bank_t[odb], big_t[0]], writes=[big_t[0]])
                P.op("vector", lambda e: e.tensor_tensor(out=denv, in0=bank[odb][:, 128:256], in1=denv, op=ALU.add),
                     reads=[bank_t[odb], big_t[1]], writes=[big_t[1]])

        flushA()
        pendA[0] = second

    def a_combine(ha, c):
        cs_ = slice(c * 512, (c + 1) * 512)
        gs_ = slice((c % 4) * 512, (c % 4 + 1) * 512)
        P.op("vector", lambda e: e.reciprocal(out=den[:, cs_], in_=den[:, cs_]), reads=[big_t[1]], writes=[big_t[1]])
        P.op("vector", lambda e: e.tensor_tensor(out=num[:, cs_], in0=num[:, cs_], in1=den[:, cs_], op=ALU.mult),
             reads=[big_t[0], big_t[1]], writes=[big_t[0]])
        P.op("vector", lambda e: e.tensor_tensor(out=ybc[0][:], in0=num[:, cs_], in1=gA[:, gs_], op=ALU.mult),
             reads=[big_t[0], gA_t], writes=[ybc_t[0]])
        yt = T("ys")
        ys_t[(8 + ha, c)] = yt
        P.op("sync", lambda e: e.dma_start(out=ys_dst(8 + ha, c), in_=ybc[0][:]),
             reads=[ybc_t[0]], writes=[yt], dma=True)
        if c % 4 == 3:
            exchange(8 + ha, c // 4)

    def a_group(ha, g, t0):
        dil, per = DILS[g], PERIOD[g]
        L = S // dil
        W = 512 // dil
        LP = per * W
        NB = LP // 128
        nblk = L // 128
        ntile = 4 if g == 2 else 3
        wsA = get_w(t0 * 128, 256)
        wsB = get_w((t0 + 2) * 128, 128 * (ntile - 2))
        kcm = kT.rearrange("p (r l) -> p r l", r=dil)
        qpm = qp[:, 0:dil * LP].rearrange("p (r l) -> p r l", r=dil)
        vpm = vTp[:, 0:dil * LP].rearrange("p (r l) -> p r l", r=dil)

        def a_chunk(c):
            hs = get_h(c)
            if c == 4 and t0 + ntile < NT2:
                prefetch_w((t0 + ntile) * 128, 256)
            cc = c % per
            pi_ = c // per
            cs_ = slice(c * 512, (c + 1) * 512)
            P.op("sync", lambda e: e.dma_start(out=cst[0], in_=cs_d[:, :, cs_].rearrange("a p t -> p a t")),
                 writes=[cst_t[0]], dma=True)
            bk = mm_tile(wsA, 0, hs)
            qk_norm(bk, t0, 2 + g, qn[0], qn_t[0])
            rope(dil, qn[0], qn_t[0], qpm[:, :, cc * W:(cc + 1) * W].rearrange("p r m -> p m r"), qp_t)
            bk = mm_tile(wsA, 1, hs)
            qk_norm(bk, t0 + 1, 5 + g, qn[0], qn_t[0])
            rope(dil, qn[0], qn_t[0], kcm[:, :, c * W:(c + 1) * W].rearrange("p r m -> p m r"), kT_t)
            bkv = mm_tile(wsB, 0, hs)
            P.op("scalar", lambda e: e.activation(
                out=vpm[:, :, cc * W:(cc + 1) * W].rearrange("p r m -> p m r"),
                in_=bank[bkv][:].rearrange("p (m r) -> p m r", r=dil), func=AF.Identity, bias=b2T[:, t0 + 2:t0 + 3]),
                reads=[bank_t[bkv], const_t], writes=[vTp_t])
            if g == 2:
                bkg = mm_tile(wsB, 1, hs)
                gs_ = slice(cc * 512, (cc + 1) * 512)
                P.op("scalar", lambda e: e.activation(out=gA[:, gs_], in_=bank[bkg][:], func=AF.Silu, bias=b2T[:, t0 + 3:t0 + 4]),
                     reads=[bank_t[bkg], const_t], writes=[gA_t])
            if cc != per - 1:
                return
            nb0 = pi_ * NB
            for r in range(dil):
                for nbl in range(NB):
                    n = nb0 + nbl
                    P.op("tensor", lambda e, r=r, nbl=nbl: e.transpose(
                        out=pbB[:, 0:128], in_=vpm[:, r, nbl * 128:(nbl + 1) * 128], identity=identB[:]),
                        reads=[vTp_t, const_t], writes=[pbB_t])
                    P.op("vector", lambda e, r=r, n=n: e.tensor_copy(out=Vt[:, r * nblk + n, :], in_=pbB[:, 0:128]),
                         reads=[pbB_t], writes=[Vt_t])
            for r in range(dil):
                for nbl in range(NB):
                    a_block(g, dil, nblk, kcm, qpm, r, nbl, nb0 + nbl)
            flushA()
            if g == 2:
                for c2_ in range(4 * pi_, 4 * pi_ + 4):
                    a_combine(ha, c2_)

        for c in range(NCH):
            a_chunk(c)

    P.op("vector", lambda e: e.memset(rt[0][:, 0:1], 0.0), reads=[const_t],
         writes=[a2_t, Fb_t[0], tS_t[0], tS_t[1], cst_t[0], qn_t[0], rt_t[0]])
    for ha in range(4):
        for g in range(3):
            t0 = tile_box[0]
            tile_box[0] += (4 if g == 2 else 3)
            a_group(ha, g, t0)
    assert tile_box[0] == NT2

    if upto <= 4:
        return finish()
    wab_v = wab_d.rearrange("(k p) n -> p k n", p=128)
    mTb = [big[0][:].bitcast(BF16).rearrange("p (j t) -> p j t", t=512),
           big[1][:].bitcast(BF16).rearrange("p (j t) -> p j t", t=512)]
    a2all_t = [a2_t, cst_t[0], qn_t[0], rt_t[0]]

    def mslice(j):
        return mTb[j // 16][:, j % 16, :], big_t[j // 16]

    def o_pair(c, hs, jp):
        wsG = [load_w(wmg_v, (4 * jp) * 128, 256), load_w(wmg_v, (4 * jp + 2) * 128, 256)]
        P.op("gpsimd", lambda e: e.dma_start(out=wabs, in_=wab_v[:, :, jp * 256:(jp + 1) * 256]),
             writes=[wabs_t] + arena_t, dma=True)
        ka = [r * 12 + u for r in range(2) for u in range(8, 12)]
        kb = [r * 12 + u for r in range(2) for u in range(8)]
        pbanks = ((AUX, SB_), (OB_, DB_))
        for jj in range(2):
            for (ks, bk) in ((ka, pbanks[jj][0]), (kb, pbanks[jj][1])):
                for ii, kc in enumerate(ks):
                    P.op("tensor", lambda e, kc=kc, ii=ii, ks=ks, bk=bk, jj=jj: e.matmul(
                        bank[bk][:], lhsT=wabs[:, kc, jj * 128:(jj + 1) * 128], rhs=ych[:, kc, :],
                        start=(ii == 0), stop=(ii == len(ks) - 1)),
                        reads=[wabs_t, ych_t, ych_pt[kc]], writes=[bank_t[bk]])
        for jj in range(2):
            j = 2 * jp + jj
            ba, bb = pbanks[jj]
            bga = mm_tile(wsG[jj], 0, hs)
            bgb = mm_tile(wsG[jj], 1, hs)
            P.op("scalar", lambda e, bga=bga, j=j: e.activation(out=zt[0][:], in_=bank[bga][:], func=AF.Sigmoid,
                                                                bias=bmgT[:, 2 * j:2 * j + 1]),
                 reads=[bank_t[bga], const_t], writes=[zt_t[0]])
            P.op("scalar", lambda e, bgb=bgb, j=j: e.activation(out=sd[0][:], in_=bank[bgb][:], func=AF.Sigmoid,
                                                                bias=bmgT[:, 2 * j + 1:2 * j + 2]),
                 reads=[bank_t[bgb], const_t], writes=[sd_t[0]])
            P.op("vector", lambda e, ba=ba: e.tensor_tensor(out=zt[0][:], in0=bank[ba][:], in1=zt[0][:], op=ALU.mult),
                 reads=[bank_t[ba], zt_t[0]], writes=[zt_t[0]])
            P.op("vector", lambda e, bb=bb: e.tensor_tensor(out=sd[0][:], in0=bank[bb][:], in1=sd[0][:], op=ALU.mult),
                 reads=[bank_t[bb], sd_t[0]], writes=[sd_t[0]])
            ms_, mt_ = mslice(j)
            P.op("vector", lambda e, ms_=ms_: e.tensor_tensor(out=ms_, in0=zt[0][:], in1=sd[0][:], op=ALU.add),
                 reads=[zt_t[0], sd_t[0]], writes=[mt_])

    def o_out(c, ip):
        wsO = load_w(wo_v, ip * 256, 256)
        for ii in range(2):
            i_ = 2 * ip + ii
            oi = i_ % 2
            P.op("sync", lambda e, oi=oi, i_=i_: e.dma_start(
                out=xo_s[oi], in_=xo_d[c * 512:(c + 1) * 512, i_ * 128:(i_ + 1) * 128].rearrange("(t p) n -> p t n", p=128)),
                writes=[xo_t[oi]] + a2all_t, dma=True)
            for k in range(32):
                ms_, mt_ = mslice(k)
                P.op("tensor", lambda e, k=k, ii=ii, ms_=ms_: e.matmul(
                    bank[OB_][:], lhsT=wslot[wsO][:, k, ii * 128:(ii + 1) * 128], rhs=ms_,
                    start=(k == 0), stop=(k == 31)),
                    reads=[wslot_t[wsO], mt_], writes=[bank_t[OB_]])
            P.op("scalar", lambda e, i_=i_: e.activation(out=zt[0][:], in_=bank[OB_][:], func=AF.Identity,
                                                         scale=gateT[:, i_:i_ + 1]),
                 reads=[bank_t[OB_], mod_t], writes=[zt_t[0]])
            for q in range(4):
                P.op("tensor", lambda e, q=q: e.transpose(out=bank[DB_][:, q * 128:(q + 1) * 128],
                                                          in_=zt[0][:, q * 128:(q + 1) * 128], identity=identF[:]),
                     reads=[zt_t[0], const_t], writes=[bank_t[DB_]])
            P.op("vector", lambda e, oi=oi: e.tensor_tensor(
                out=ob_s[oi], in0=bank[DB_][:].rearrange("p (t n) -> p t n", n=128), in1=xo_s[oi], op=ALU.add),
                reads=[bank_t[DB_], xo_t[oi]], writes=[ob_t[oi]])
            P.op("sync", lambda e, oi=oi, i_=i_: e.dma_start(
                out=out_d[c * 512:(c + 1) * 512, i_ * 128:(i_ + 1) * 128].rearrange("(t p) n -> p t n", p=128), in_=ob_s[oi]),
                reads=[ob_t[oi]], writes=[T("outd")], dma=True)

    ych_pt = [T("ychp%d" % k) for k in range(24)]
    tmp_pt = [T("tmpp%d" % k) for k in range(24)]

    def o_loads(c):
        load_h(hTo_d, "o", c, hs=0)
        tmp = hslot[1][:, 0:24, :]
        tmp_t = hslot_t[1]
        for r in range(2):
            for u in range(12):
                kc = r * 12 + u
                P.op("sync", lambda e, r=r, u=u, kc=kc: e.dma_start(
                    out=ych[:, kc, :], in_=ya_p[u][0].ap()[r * 128:(r + 1) * 128, c * 512:(c + 1) * 512]),
                    reads=[ya_t[u][0]], writes=[ych_pt[kc]] + (arena_t + [ych_t] if kc == 0 else []), dma=True)
                P.op("sync", lambda e, r=r, u=u, kc=kc: e.dma_start(
                    out=tmp[:, kc, :], in_=ya_p[u][1].ap()[r * 128:(r + 1) * 128, c * 512:(c + 1) * 512]),
                    reads=[ya_t[u][1]], writes=[tmp_pt[kc]] + ([tmp_t] if kc == 0 else []), dma=True)

    def o_blend():
        tmp = hslot[1][:, 0:24, :]
        tmp_t = hslot_t[1]
        P.op("vector", lambda e: e.tensor_scalar(out=tmp, in0=tmp, scalar1=msk[:, 1:2], scalar2=None, op0=ALU.mult),
             reads=[tmp_t, const_t] + tmp_pt, writes=[tmp_t] + tmp_pt)
        P.op("vector", lambda e: e.scalar_tensor_tensor(out=ych, in0=ych, scalar=msk[:, 0:1], in1=tmp, op0=ALU.mult, op1=ALU.add),
             reads=[ych_t, tmp_t, const_t] + tmp_pt + ych_pt, writes=[ych_t] + ych_pt)

    def o_chunk(c):
        if c == 0:
            o_loads(0)
            o_blend()
        for jp in range(16):
            o_pair(c, 0, jp)
        if c + 1 < NCH // 2:
            o_loads(c + 1)
        for ip in range(16):
            o_out(c, ip)
        if c + 1 < NCH // 2:
            o_blend()

    for c in range(NCH // 2):
        o_chunk(c)

    return finish()


A_Q0, A_K0, A_V0, A_G0 = 0, 3072, 6144, 9216
B_Q0, B_K0, B_V0, B_G0, B_F0 = 10240, 12288, 14336, 16384, 18432
M_G0 = 18448


def _consts():
    half = 64
    inv = 10000.0 ** (-np.arange(half, dtype=np.float64) / half)
    ang = np.arange(S, dtype=np.float64)[None, :] * np.concatenate([inv, inv])[:, None]
    cos = np.cos(ang)
    sin = np.sin(ang)
    sgn = np.concatenate([-np.ones(half), np.ones(half)])[:, None]
    cs = np.stack([cos, sgn * sin]).astype(np.float32)
    identF = np.eye(128, dtype=np.float32)
    selF = np.zeros((8, 8, 128), np.float32)
    for h in range(8):
        selF[h, h, :] = 1.0
    k = np.arange(128)[:, None]
    q = np.arange(512)[None, :]
    q1 = np.arange(128)[None, :]
    maskB = np.where(q1 >= k, 0.0, NEG).astype(np.float32)
    maskA = np.concatenate([(q1 <= k), (q1 >= k)], axis=1).astype(np.float32)
    return cs, identF, selF.reshape(8, 1024), maskB, maskA


def _tm(v):
    return np.ascontiguousarray(v.reshape(-1, 128).T)


_NC = None
_UPTO = 9
_DBG = False
_RAW = False


def kernel(x, c, norm_g, w_ada, b_ada, w_in, b_in, a_q_norm, a_k_norm, b_q_norm, b_k_norm,
           w_a_out, w_b_out, w_o):
    global _NC
    if _NC is None:
        _NC = build(_UPTO, _DBG)
    nc = _NC
    f = np.float32
    x = np.asarray(x, f); c = np.asarray(c, f)
    w_in0 = np.asarray(w_in, f)[0]; b_in0 = np.asarray(b_in, f)[0]
    cs, identF, selF, maskB, maskA = _consts()
    wada = np.ascontiguousarray(np.asarray(w_ada, f)[0])
    badaT = _tm(np.asarray(b_ada, f)[0])
    gT = _tm(np.asarray(norm_g, f)[0])
    wo = np.ascontiguousarray(np.asarray(w_o, f)[0])
    wa = np.asarray(w_a_out, f)[0]; wb = np.asarray(w_b_out, f)[0]
    wab = np.concatenate([np.concatenate([wb[r * 1024:(r + 1) * 1024], wa[r * 512:(r + 1) * 512]], axis=0) for r in range(2)], axis=0)
    wab = np.ascontiguousarray(wab)
    mgcols = np.concatenate([np.concatenate([M_G0 + j * 128 + np.arange(128), M_G0 + D + j * 128 + np.arange(128)]) for j in range(32)])
    wmg = np.ascontiguousarray(w_in0[:, mgcols])
    bmgT = _tm(b_in0[mgcols])
    per_core_w = []
    for p in range(2):
        cols = []
        fc = np.concatenate([B_F0 + 8 * p + np.arange(8), np.full(120, -1)])
        cols.append(fc)
        for j in range(8):
            h = 8 * p + j
            for base in (B_Q0, B_K0, B_V0, B_G0):
                cols.append(base + h * 128 + np.arange(128))
        for j in range(4):
            h = 4 * p + j
            for g in range(3):
                for base in (A_Q0, A_K0, A_V0):
                    cols.append(base + (g * 8 + h) * 128 + np.arange(128))
                if g == 2:
                    cols.append(A_G0 + h * 128 + np.arange(128))
        cols = np.concatenate(cols)
        assert cols.shape[0] == NT2 * 128
        w2 = w_in0[:, np.maximum(cols, 0)].copy()
        w2[:, cols < 0] = 0.0
        b2 = b_in0[np.maximum(cols, 0)].copy()
        b2[cols < 0] = 0.0
        per_core_w.append((np.ascontiguousarray(w2), _tm(b2)))
    gains = np.stack([np.asarray(b_q_norm, f)[0], np.asarray(b_k_norm, f)[0],
                      np.asarray(a_q_norm, f)[0, 0], np.asarray(a_q_norm, f)[0, 1], np.asarray(a_q_norm, f)[0, 2],
                      np.asarray(a_k_norm, f)[0, 0], np.asarray(a_k_norm, f)[0, 1], np.asarray(a_k_norm, f)[0, 2]], axis=1)
    gains = np.ascontiguousarray(gains.astype(f))
    in_maps = []
    for core in range(8):
        b, p = core // 2, core % 2
        w2, b2T = per_core_w[p]
        m = np.zeros((128, 2), f)
        m[:, p] = 1.0
        in_maps.append({
            "x": np.ascontiguousarray(x[b]), "xo": np.ascontiguousarray(x[b, p * 2048:(p + 1) * 2048]),
            "cT": _tm(c[b]), "wada": wada, "badaT": badaT, "gT": gT, "w2": w2, "b2T": b2T,
            "wmg": wmg, "bmgT": bmgT, "wab": wab, "wo": wo, "gains": gains, "cs": cs, "msk": m,
            "identF": identF, "selF": selF, "maskT": maskB, "maskA": maskA,
        })
    if _RAW:
        return run_bass_kernel_spmd(nc, in_maps, core_ids=list(range(8)))
    res = run_bass_kernel_spmd(nc, in_maps, core_ids=list(range(8)))
    out = np.empty((4, S, D), f)
    for core in range(8):
        b, p = core // 2, core % 2
        out[b, p * 2048:(p + 1) * 2048] = res.results[core]["out"]
    return out
```

```python
import contextlib
import numpy as np
import concourse.bass as bass
import concourse.mybir as mybir
from concourse.bass_utils import run_bass_kernel_spmd

F32 = mybir.dt.float32
BF16 = mybir.dt.bfloat16
ALU = mybir.AluOpType
AF = mybir.ActivationFunctionType

D = 4096
S = 4096
NCH = 8
EPS = 1e-6
SCALE = 128 ** -0.5
DILS = (1, 4, 16)
PERIOD = (1, 1, 4)
NT2 = 1 + 8 * 4 + 4 * 10
NEG = -30000.0


class T:
    __slots__ = ("name", "w", "r")

    def __init__(self, name):
        self.name = name
        self.w = None
        self.r = []


class I:
    __slots__ = ("eng", "fn", "deps", "signal", "dma", "sem", "val", "pos", "cc")


class Prog:
    ENGS = ("tensor", "vector", "scalar", "gpsimd", "sync")

    def __init__(self):
        self.streams = {e: [] for e in self.ENGS}

    def op(self, eng, fn, reads=(), writes=(), dma=False, cc=False):
        i = I()
        i.eng, i.fn, i.dma, i.signal, i.cc = eng, fn, dma, False, cc
        i.sem = i.val = None
        i.pos = len(self.streams[eng])
        deps = []
        for t in reads:
            if t.w is not None:
                deps.append(t.w)
        for t in writes:
            if t.w is not None:
                deps.append(t.w)
            deps.extend(t.r)
        best = {}
        out = []
        for d in deps:
            if d is i:
                continue
            if d.dma or d.cc:
                out.append(d)
            else:
                if d.eng == "tensor" and eng == "tensor":
                    continue
                b = best.get(d.eng)
                if b is None or d.pos > b.pos:
                    best[d.eng] = d
        out.extend(best.values())
        i.deps = out
        for d in out:
            d.signal = True
        for t in reads:
            if not (dma or cc):
                t.r = [x for x in t.r if x.dma or x.cc or x.eng != eng]
            t.r.append(i)
        for t in writes:
            t.w = i
            t.r = []
        self.streams[eng].append(i)
        return i

    def emit(self, nc, es):
        NPOOL = 12
        csem = {e: es.enter_context(nc.semaphore("c_" + e)) for e in self.ENGS}
        dsem = {e: [es.enter_context(nc.semaphore("d_%s%d" % (e, k))) for k in range(NPOOL)]
                for e in ("sync", "gpsimd")}
        ccsem = es.enter_context(nc.semaphore("ccs"))
        for e in self.ENGS:
            cnt = 0
            dcnt = [0] * NPOOL
            rr = 0
            ccn = 0
            for i in self.streams[e]:
                if i.cc:
                    ccn += 1
                    i.sem, i.val = ccsem, ccn
                    i.signal = True
                elif i.dma:
                    k = rr % NPOOL
                    rr += 1
                    dcnt[k] += 16
                    i.sem, i.val = dsem[e][k], dcnt[k]
                    i.signal = True
                elif i.signal:
                    cnt += 1
                    i.sem, i.val = csem[e], cnt
        block = es.enter_context(nc.Block())

        def run(e):
            def body(eng):
                waited = {}
                for i in self.streams[e]:
                    need = {}
                    for d in i.deps:
                        k = id(d.sem)
                        if waited.get(k, 0) >= d.val:
                            continue
                        if k not in need or need[k][1] < d.val:
                            need[k] = (d.sem, d.val)
                    for k, (s, v) in need.items():
                        eng.wait_ge(s, v)
                        waited[k] = v
                    ins = i.fn(eng)
                    if i.cc:
                        ins.then_inc(i.sem)
                    elif i.dma:
                        ins.then_inc(i.sem, 16)
                    elif i.signal:
                        ins.then_inc(i.sem, 1)
                if e in dsem:
                    last = {}
                    for i in self.streams[e]:
                        if i.dma or i.cc:
                            last[id(i.sem)] = (i.sem, i.val)
                    for s, v in last.values():
                        eng.wait_ge(s, v)
            return body

        block.tensor(run("tensor"))
        block.vector(run("vector"))
        block.scalar(run("scalar"))
        block.gpsimd(run("gpsimd"))
        block.sync(run("sync"))


def build(upto=9, dbg=False):
    nc = bass.Bass("TRN2", target_bir_lowering=False)
    es = contextlib.ExitStack()
    P = Prog()

    def din(name, shape, dt=F32):
        return nc.dram_tensor(name, list(shape), dt, kind="ExternalInput").ap()

    x_d = din("x", [S, D])
    xo_d = din("xo", [S // 2, D])
    cT_d = din("cT", [128, 32])
    wada_d = din("wada", [D, 3 * D])
    badaT_d = din("badaT", [128, 96])
    gT_d = din("gT", [128, 32])
    w2_d = din("w2", [D, NT2 * 128])
    b2T_d = din("b2T", [128, NT2])
    wmg_d = din("wmg", [D, 2 * D])
    bmgT_d = din("bmgT", [128, 64])
    wab_d = din("wab", [24 * 128, D])
    wo_d = din("wo", [D, D])
    gains_d = din("gains", [128, 8])
    cs_d = din("cs", [2, 128, S])
    msk_d = din("msk", [128, 2])
    identF_d = din("identF", [128, 128])
    selF_d = din("selF", [128, 8 * 128])
    maskT_d = din("maskT", [128, 128])
    maskA_d = din("maskA", [128, 256])
    out_d = nc.dram_tensor("out", [S // 2, D], F32, kind="ExternalOutput").ap()

    if dbg:
        dbg_mod = nc.dram_tensor("dbg_mod", [128, 96], F32, kind="ExternalOutput").ap()
        dbg_F = nc.dram_tensor("dbg_F", [8, S], F32, kind="ExternalOutput").ap()
        dbg_Ftok = nc.dram_tensor("dbg_Ftok", [128, 256], F32, kind="ExternalOutput").ap()
        dbg_ys = nc.dram_tensor("dbg_ys", [12 * 128, S], BF16, kind="ExternalOutput").ap()
        dbg_h = nc.dram_tensor("dbg_h", [128, 32 * 512], BF16, kind="ExternalOutput").ap()
    hT_d = nc.dram_tensor("hT_d", [NCH, 128, 32 * 512], BF16)
    hTo_d = nc.dram_tensor("hTo_d", [NCH // 2, 128, 32 * 512], BF16)
    ys_p = [[nc.dram_tensor("ys_%d_%d" % (u, hf), [128, S // 2], BF16) for hf in range(2)] for u in range(12)]
    ya_p = [[nc.dram_tensor("ya_%d_%d" % (u, hf), [256, S // 2], BF16) for hf in range(2)] for u in range(12)]
    ya_t = [[T("ya_%d_%d" % (u, hf)) for hf in range(2)] for u in range(12)]

    def ys_dst(u, c):
        return ys_p[u][c // 4].ap()[:, (c % 4) * 512:(c % 4 + 1) * 512]

    def exchange(u, hf):
        P.op("gpsimd", lambda e: e.collective_compute("AllGather", ALU.bypass, replica_groups=[[0, 1], [2, 3], [4, 5], [6, 7]],
                                                      ins=[ys_p[u][hf].ap().opt()], outs=[ya_p[u][hf].ap().opt()]),
             reads=[ys_t[(u, c)] for c in range(4 * hf, 4 * hf + 4)], writes=[ya_t[u][hf]], cc=True)

    def sb(name, shape, dt):
        return es.enter_context(nc.sbuf_tensor(name + "_s", list(shape), dt))

    def pst(name, shape, dt):
        return es.enter_context(nc.psum_tensor(name, list(shape), dt))

    wslot = [sb("wslot%d" % k, [128, 32, 256], BF16) for k in range(3)]
    wslot_t = [T("wslot%d" % k) for k in range(3)]
    hslot = [sb("hslot%d" % k, [128, 32, 512], BF16) for k in range(2)]
    hslot_t = [T("hslot%d" % k) for k in range(2)]
    big = [sb("big%d" % k, [128, 4096], F32) for k in range(2)]
    big_t = [T("big%d" % k) for k in range(2)]
    arena = sb("arena", [128, 18 * 1024], BF16)
    arena2 = sb("arena2", [128, 5 * 1024], BF16)
    off = [0]

    def carve(nelem_bf16, ar=None, lim=18 * 1024):
        ar = arena if ar is None else ar
        a = ar[:, off[0]:off[0] + nelem_bf16]
        off[0] += nelem_bf16
        assert off[0] <= lim, off[0]
        return a

    def c2(nelem_bf16):
        return carve(nelem_bf16, arena2, 5 * 1024)

    kT = carve(4096)
    Vt = carve(4096).rearrange("p (t d) -> p t d", d=128)
    gA = carve(2048)
    qp = carve(2048)
    vTp = carve(2048)
    kT_t, Vt_t, gA_t, qp_t, vTp_t = T("kT"), T("Vt"), T("gA"), T("qp"), T("vTp")
    arena_t = [kT_t, Vt_t, gA_t, qp_t, vTp_t]
    off[0] = 0
    ych = carve(24 * 512).rearrange("p (k t) -> p k t", t=512)
    wabs = carve(24 * 256).rearrange("p (k n) -> p k n", n=256)
    ych_t, wabs_t = T("ych"), T("wabs")
    mT_t = T("mT")
    a2_t = T("arena2")
    off[0] = 0
    sel8 = c2(1024)
    tS = [c2(1024).bitcast(F32), c2(1024).bitcast(F32)]
    selF = c2(2048).bitcast(F32)
    off[0] = 0
    cst = [c2(2048).bitcast(F32).rearrange("p (a t) -> p a t", a=2)]
    qn = [c2(1024).bitcast(F32), None]
    rt = [c2(1024).bitcast(F32), None]
    off[0] = 0
    xo_s = [c2(1024).bitcast(F32).rearrange("p (t n) -> p t n", n=128) for k in range(2)]
    ob_s = [c2(1024).bitcast(F32).rearrange("p (t n) -> p t n", n=128) for k in range(2)]
    Fb_t = [T('sel8')]
    tS_t = [T('tS0'), T('tS1')]
    cst_t = [T("cst")]
    qn_t = [T("qn0"), None]
    rt_t = [T("rt0"), None]
    xo_t = [T("xo0"), T("xo1")]
    ob_t = [T("ob0"), T("ob1")]

    identF = sb("identF", [128, 128], F32)
    identB = sb("identB", [128, 128], BF16)
    onesB = sb("onesB", [128, 128], BF16)
    maskT = sb("maskT", [128, 128], F32)
    maskA = sb("maskA", [128, 256], BF16)
    cT = sb("cT", [128, 32], F32)
    scB = sb("scB", [128, 32], BF16)
    modT = sb("modT", [128, 96], F32)
    badaT = sb("badaT", [128, 96], F32)
    gT = sb("gT", [128, 32], F32)
    sc1 = sb("sc1", [128, 32], F32)
    b2T = sb("b2T", [128, NT2], F32)
    bmgT = sb("bmgT", [128, 64], F32)
    gains = sb("gains", [128, 8], F32)
    msk = sb("msk", [128, 2], F32)
    Ftok = sb("Ftok", [128, 32 * 8], F32)
    zt = [sb("zt0", [128, 512], F32)]
    zt_t = [T("zt0")]
    sqb = [sb("sqb0", [128, 512], BF16)]
    sqb_t = [T("sqb0")]
    sd = [sb("sd0", [128, 512], F32)]
    sd_t = [T("sd0")]
    qc = [sb("qc%d" % k, [128, 512], BF16) for k in range(1)]
    qc_t = [T("qc%d" % k) for k in range(1)]
    gc = [sb("gc%d" % k, [128, 512], BF16) for k in range(1)]
    gc_t = [T("gc%d" % k) for k in range(1)]
    vTc = [sb("vTc%d" % k, [128, 512], BF16) for k in range(1)]
    vTc_t = [T("vTc%d" % k) for k in range(1)]
    PT = [sb("PT%d" % k, [128, 512], BF16) for k in range(3)]
    PT_t = [T("PT%d" % k) for k in range(3)]
    PTs_t = [T("PTs%d" % k) for k in range(12)]
    Ss_t = [T("Ss%d" % k) for k in range(8)]
    Os_t = [T("Os%d" % k) for k in range(4)]
    Ds_t = [T("Ds%d" % k) for k in range(4)]
    ybc = [sb("ybc%d" % k, [128, 512], BF16) for k in range(1)]
    ybc_t = [T("ybc%d" % k) for k in range(1)]
    ss = sb("ss", [128, 4], F32)
    ss_t = T("ss")
    const_t = T("const")
    mod_t = T("mod")
    Ftok_t = T("Ftok")

    bank = [pst("bank%d" % k, [128, 512], F32) for k in range(7)]
    bank_t = [T("bank%d" % k) for k in range(7)]
    pbB = pst("pbB", [128, 1024], BF16)
    pbB_t = T("pbB")
    IN = [0, 1]
    SB2_ = 2
    AUX, SB_, OB_, DB_ = 3, 4, 5, 6

    rr = {"in": 0, "pt": 0, "w": 0, "h": 0, "ss": 0, "od": 0, "pts": 0}
    ys_t = {}

    def finish():
        if dbg:
            P.op("sync", lambda e: e.dma_start(out=dbg_mod, in_=modT[:]), reads=[mod_t], writes=[T("d1")], dma=True)
            if upto >= 1:
                P.op("sync", lambda e: e.dma_start(out=dbg_h, in_=hT_d[0]), reads=[hd_t[("a", 0)]], writes=[T("d2")], dma=True)
            if upto >= 3:
                for u in range(12):
                    for hf in range(2):
                        P.op("sync", lambda e, u=u, hf=hf: e.dma_start(out=dbg_ys[u * 128:(u + 1) * 128, hf * 2048:(hf + 1) * 2048],
                                                                       in_=ys_p[u][hf].ap()),
                             reads=[ys_t[(u, c)] for c in range(4 * hf, 4 * hf + 4) if (u, c) in ys_t], writes=[T("d3")], dma=True)
        P.emit(nc, es)
        es.close()
        return nc

    def nxt(k, n):
        v = rr[k] % n
        rr[k] += 1
        return v

    def ld(dst, src, t, eng="sync"):
        P.op(eng, lambda e: e.dma_start(out=dst, in_=src), writes=[t], dma=True)

    for dst, src in ((identF[:], identF_d), (maskT[:], maskT_d), (zt[0][:, 0:256], maskA_d),
                     (cT[:], cT_d), (badaT[:], badaT_d), (gT[:], gT_d),
                     (b2T[:], b2T_d), (bmgT[:], bmgT_d), (gains[:], gains_d), (msk[:], msk_d)):
        ld(dst, src, const_t)
    P.op("vector", lambda e: e.tensor_copy(out=identB[:], in_=identF[:]), reads=[const_t], writes=[const_t])
    P.op("vector", lambda e: e.memset(onesB[:], 1.0), writes=[const_t])
    P.op("vector", lambda e: e.tensor_copy(out=maskA[:], in_=zt[0][:, 0:256]), reads=[const_t], writes=[const_t, zt_t[0]])
    P.op("scalar", lambda e: e.activation(out=scB[:], in_=cT[:], func=AF.Silu), reads=[const_t], writes=[const_t])

    wada_v = wada_d.rearrange("(k p) n -> p k n", p=128)

    w512 = [wslot[k][:].rearrange("p k n -> p (k n)").rearrange("p (k n) -> p k n", n=512) for k in range(3)]
    w512 += [hslot[1][:, 0:16, :], hslot[1][:, 16:32, :]]

    def ph0(cg4):
        grp = cg4 % 2
        if grp == 0:
            sl, tl_ = (w512[0], w512[1]), (wslot_t[0], wslot_t[1])
        else:
            sl, tl_ = (w512[3], w512[4]), (hslot_t[1], hslot_t[1])
        for kh in range(2):
            P.op("gpsimd", lambda e, kh=kh: e.dma_start(out=sl[kh], in_=wada_v[:, kh * 16:(kh + 1) * 16, cg4 * 512:(cg4 + 1) * 512]),
                 writes=[tl_[kh]], dma=True)
        for t in range(4):
            ct = cg4 * 4 + t
            for k in range(32):
                P.op("tensor", lambda e, t=t, k=k, ct=ct: e.matmul(
                    bank[AUX][:, ct:ct + 1], lhsT=sl[k // 16][:, k % 16, t * 128:(t + 1) * 128], rhs=scB[:, k:k + 1],
                    start=(k == 0), stop=(k == 31)),
                    reads=[tl_[k // 16], const_t], writes=[bank_t[AUX]])
    for cg4 in range(24):
        ph0(cg4)
    P.op("vector", lambda e: e.tensor_tensor(out=modT[:], in0=bank[AUX][:, 0:96], in1=badaT[:], op=ALU.add),
         reads=[bank_t[AUX], const_t], writes=[mod_t])
    P.op("vector", lambda e: e.scalar_tensor_tensor(out=sc1[:], in0=modT[:, 32:64], scalar=1.0, in1=gT[:],
                                                    op0=ALU.add, op1=ALU.mult),
         reads=[mod_t, const_t], writes=[mod_t])
    shiftT = modT[:, 0:32]
    gateT = modT[:, 64:96]

    if upto <= 0:
        return finish()
    junk = arena[:, 0:4096]
    xbf = arena[:, 4096:8192]
    banksB = [(bank[IN[0]][:].bitcast(BF16), bank_t[IN[0]]), (bank[IN[1]][:].bitcast(BF16), bank_t[IN[1]]), (pbB[:], pbB_t)]
    junk_t = kT_t
    hd_t = {}
    for ch in range(NCH):
        hd_t[("a", ch)] = T("hTd%d" % ch)
    for ch in range(NCH // 2):
        hd_t[("o", ch)] = T("hTod%d" % ch)

    def h_tile(src, dst, key, tt):
        xb = tt % 2
        xt = big[xb]
        tl = tt % 4
        ch = tt // 4
        hs = ch % 2
        P.op("sync", lambda e: e.dma_start(out=xt[:], in_=src[tt * 128:(tt + 1) * 128, :]),
             writes=[big_t[xb]], dma=True)
        P.op("scalar", lambda e: e.activation(out=junk, in_=xt[:], func=AF.Square, accum_out=ss[:, 0:1]),
             reads=[big_t[xb]], writes=[junk_t, ss_t])
        P.op("scalar", lambda e: e.activation(out=ss[:, 1:2], in_=ss[:, 0:1], func=AF.Sqrt, scale=1.0 / D, bias=EPS),
             reads=[ss_t], writes=[ss_t])
        P.op("vector", lambda e: e.reciprocal(out=ss[:, 2:3], in_=ss[:, 1:2]), reads=[ss_t], writes=[ss_t])
        P.op("vector", lambda e: e.tensor_scalar(out=xbf, in0=xt[:], scalar1=ss[:, 2:3], scalar2=None, op0=ALU.mult),
             reads=[ss_t, big_t[xb]], writes=[Vt_t])
        for kg in range(4):
            bi_ = nxt("in", 3)
            bkB, bkB_t = banksB[bi_]
            for q in range(8):
                kc = kg * 8 + q
                P.op("tensor", lambda e, kc=kc, q=q, bkB=bkB: e.transpose(
                    out=bkB[:, q * 128:(q + 1) * 128], in_=xbf[:, kc * 128:(kc + 1) * 128], identity=identB[:]),
                    reads=[Vt_t, const_t], writes=[bkB_t])
            for q in range(8):
                kc = kg * 8 + q
                if q % 2 == 0:
                    P.op("vector", lambda e, kc=kc, q=q, bkB=bkB: e.tensor_scalar(
                        out=hslot[hs][:, kc, tl * 128:(tl + 1) * 128], in0=bkB[:, q * 128:(q + 1) * 128],
                        scalar1=sc1[:, kc:kc + 1], scalar2=shiftT[:, kc:kc + 1], op0=ALU.mult, op1=ALU.add),
                        reads=[bkB_t, mod_t], writes=[hslot_t[hs]])
                else:
                    P.op("scalar", lambda e, kc=kc, q=q, bkB=bkB: e.activation(
                        out=hslot[hs][:, kc, tl * 128:(tl + 1) * 128], in_=bkB[:, q * 128:(q + 1) * 128],
                        func=AF.Identity, scale=sc1[:, kc:kc + 1], bias=shiftT[:, kc:kc + 1]),
                        reads=[bkB_t, mod_t], writes=[hslot_t[hs]])
        if tl == 3:
            P.op("sync", lambda e: e.dma_start(out=dst[ch], in_=hslot[hs][:].rearrange("p k t -> p (k t)")),
                 reads=[hslot_t[hs]], writes=[hd_t[(key, ch)]], dma=True)

    for tt in range(32):
        h_tile(x_d, hT_d, "a", tt)
    for tt in range(16):
        h_tile(xo_d, hTo_d, "o", tt)

    if upto <= 1:
        return finish()
    w2_v = w2_d.rearrange("(k p) n -> p k n", p=128)
    wmg_v = wmg_d.rearrange("(k p) n -> p k n", p=128)
    wo_v = wo_d.rearrange("(k p) n -> p k n", p=128)

    def load_w(view, col0, ncols):
        ws = nxt("w", 3)
        P.op("gpsimd", lambda e: e.dma_start(out=wslot[ws][:, :, 0:ncols], in_=view[:, :, col0:col0 + ncols]),
             writes=[wslot_t[ws]], dma=True)
        return ws

    def load_h(src, key, ch, hs=None):
        if hs is None:
            hs = nxt("h", 2)
        P.op("sync", lambda e: e.dma_start(out=hslot[hs][:].rearrange("p k t -> p (k t)"), in_=src[ch]),
             reads=[hd_t[(key, ch)]], writes=[hslot_t[hs]], dma=True)
        return hs

    hq = {"slot": None, "c": None}

    def get_h(c):
        if hq["c"] == c:
            hs = hq["slot"]
        else:
            hs = load_h(hT_d, "a", c)
        n_ = (c + 1) % NCH
        hq["slot"] = load_h(hT_d, "a", n_)
        hq["c"] = n_
        return hs

    wq = {}

    def prefetch_w(col0, ncols):
        wq[col0] = load_w(w2_v, col0, ncols)

    def get_w(col0, ncols):
        if col0 in wq:
            return wq.pop(col0)
        return load_w(w2_v, col0, ncols)

    def mm_tile(ws, wj, hs):
        bk = IN[nxt("in", 2)]
        for k in range(32):
            P.op("tensor", lambda e, k=k: e.matmul(bank[bk][:], lhsT=wslot[ws][:, k, wj * 128:(wj + 1) * 128],
                                                   rhs=hslot[hs][:, k, :], start=(k == 0), stop=(k == 31)),
                 reads=[wslot_t[ws], hslot_t[hs]], writes=[bank_t[bk]])
        return bk

    def qk_norm(bk, bcol, gcol, out_ap, out_t):
        z = 0
        P.op("scalar", lambda e: e.activation(out=zt[z][:], in_=bank[bk][:], func=AF.Identity, bias=b2T[:, bcol:bcol + 1]),
             reads=[bank_t[bk], const_t], writes=[zt_t[z]])
        P.op("scalar", lambda e: e.activation(out=sqb[z][:], in_=bank[bk][:], func=AF.Square, bias=b2T[:, bcol:bcol + 1]),
             reads=[bank_t[bk], const_t], writes=[sqb_t[z]])
        P.op("tensor", lambda e: e.matmul(bank[AUX][:], lhsT=onesB[:], rhs=sqb[z][:], start=True, stop=True),
             reads=[sqb_t[z], const_t], writes=[bank_t[AUX]])
        P.op("scalar", lambda e: e.activation(out=sd[z][:], in_=bank[AUX][:], func=AF.Sqrt, scale=1.0 / 128, bias=EPS),
             reads=[bank_t[AUX]], writes=[sd_t[z]])
        P.op("vector", lambda e: e.reciprocal(out=sd[z][:], in_=sd[z][:]), reads=[sd_t[z]], writes=[sd_t[z]])
        P.op("vector", lambda e: e.scalar_tensor_tensor(out=out_ap, in0=zt[z][:], scalar=gains[:, gcol:gcol + 1],
                                                        in1=sd[z][:], op0=ALU.mult, op1=ALU.mult),
             reads=[zt_t[z], sd_t[z], const_t], writes=[out_t])

    FT = big[0]
    lf = big[1]

    def f_chunk(ws, c):
        hs = get_h(c)
        if c == 4:
            prefetch_w(128, 256)
        bk = mm_tile(ws, 0, hs)
        cs_ = slice(c * 512, (c + 1) * 512)
        P.op("scalar", lambda e: e.activation(out=lf[0:8, cs_], in_=bank[bk][0:8, :], func=AF.Identity, bias=b2T[0:8, 0:1]),
             reads=[bank_t[bk], const_t], writes=[big_t[1]])
        P.op("scalar", lambda e: e.activation(out=FT[0:8, cs_], in_=lf[0:8, cs_], func=AF.Abs),
             reads=[big_t[1]], writes=[big_t[0]])
        P.op("scalar", lambda e: e.activation(out=FT[0:8, cs_], in_=FT[0:8, cs_], func=AF.Exp, scale=-1.0),
             reads=[big_t[0]], writes=[big_t[0]])
        P.op("scalar", lambda e: e.activation(out=FT[0:8, cs_], in_=FT[0:8, cs_], func=AF.Ln, bias=1.0),
             reads=[big_t[0]], writes=[big_t[0]])
        P.op("vector", lambda e: e.tensor_single_scalar(out=lf[0:8, cs_], in_=lf[0:8, cs_], scalar=0.0, op=ALU.min),
             reads=[big_t[1]], writes=[big_t[1]])
        P.op("vector", lambda e: e.tensor_tensor(out=lf[0:8, cs_], in0=lf[0:8, cs_], in1=FT[0:8, cs_], op=ALU.subtract),
             reads=[big_t[1], big_t[0]], writes=[big_t[1]])

    ws0 = load_w(w2_v, 0, 128)
    for c in range(NCH):
        f_chunk(ws0, c)
    P.op("vector", lambda e: e.tensor_tensor_scan(out=FT[0:8, :], data0=lf[0:8, :], data1=lf[0:8, :], initial=0.0,
                                                  op0=ALU.add, op1=ALU.bypass),
         reads=[big_t[1]], writes=[big_t[0]])
    for tt in range(32):
        P.op("tensor", lambda e, tt=tt: e.transpose(out=bank[AUX][:, 0:8], in_=FT[0:8, tt * 128:(tt + 1) * 128],
                                                    identity=identF[0:8, 0:8]),
             reads=[big_t[0], const_t], writes=[bank_t[AUX]])
        P.op("scalar", lambda e, tt=tt: e.mul(out=Ftok[:, tt * 8:(tt + 1) * 8], in_=bank[AUX][:, 0:8], mul=-1.0),
             reads=[bank_t[AUX]], writes=[Ftok_t])
    P.op("sync", lambda e: e.dma_start(out=selF, in_=selF_d), writes=[a2_t], dma=True)
    P.op("vector", lambda e: e.tensor_copy(out=sel8, in_=selF), reads=[a2_t], writes=[Fb_t[0]])
    hi128 = big[1][:].bitcast(BF16)[:, 0:S]
    hi8 = big[1][:].bitcast(BF16)[0:8, 0:S]
    P.op("vector", lambda e: e.memset(hi128, 0.0), reads=[big_t[1]], writes=[big_t[1]])
    P.op("vector", lambda e: e.tensor_copy(out=hi8, in_=FT[0:8, :]), reads=[big_t[0], big_t[1]], writes=[big_t[1]])
    P.op("vector", lambda e: e.tensor_scalar(out=gains[:, 0:1], in0=gains[:, 0:1], scalar1=SCALE, scalar2=None, op0=ALU.mult),
         reads=[const_t], writes=[const_t])
    if dbg:
        P.op("sync", lambda e: e.dma_start(out=dbg_F, in_=FT[0:8, :]), reads=[big_t[0]], writes=[T("d4")], dma=True)
        P.op("sync", lambda e: e.dma_start(out=dbg_Ftok, in_=Ftok[:]), reads=[Ftok_t], writes=[T("d5")], dma=True)
    if upto <= 2:
        return finish()
    tile_box = [1]

    def b_chunk(hb, t0, wsA, wsB, c):
        hs = get_h(c)
        if c == 4:
            prefetch_w((t0 + 4) * 128, 256)
        cs_ = slice(c * 512, (c + 1) * 512)
        bk = mm_tile(wsA, 0, hs)
        qk_norm(bk, t0, 0, qc[0][:], qc_t[0])
        bk = mm_tile(wsA, 1, hs)
        qk_norm(bk, t0 + 1, 1, kT[:, cs_], kT_t)
        bkv = mm_tile(wsB, 0, hs)
        P.op("scalar", lambda e: e.activation(out=vTc[0][:], in_=bank[bkv][:], func=AF.Identity, bias=b2T[:, t0 + 2:t0 + 3]),
             reads=[bank_t[bkv], const_t], writes=[vTc_t[0]])
        for q in range(4):
            P.op("tensor", lambda e, q=q: e.transpose(out=pbB[:, q * 128:(q + 1) * 128],
                                                      in_=vTc[0][:, q * 128:(q + 1) * 128], identity=identB[:]),
                 reads=[vTc_t[0], const_t], writes=[pbB_t])
        P.op("vector", lambda e: e.tensor_copy(out=Vt[:, 4 * c:4 * c + 4, :],
                                               in_=pbB[:, 0:512].rearrange("p (t d) -> p t d", d=128)),
             reads=[pbB_t], writes=[Vt_t])
        bkg = mm_tile(wsB, 1, hs)
        P.op("scalar", lambda e: e.activation(out=gc[0][:], in_=bank[bkg][:], func=AF.Silu, bias=b2T[:, t0 + 3:t0 + 4]),
             reads=[bank_t[bkg], const_t], writes=[gc_t[0]])
        nkt = 4 * c + 4
        pend = None

        def pv(kt, pi, q0):
            P.op("tensor", lambda e: e.matmul(bank[OB_][:, q0:512], lhsT=Vt[:, kt, :], rhs=PT[pi][:, q0:512],
                                              start=(kt == 0), stop=(kt == nkt - 1)),
                 reads=[Vt_t, PT_t[pi]], writes=[bank_t[OB_]])
            P.op("tensor", lambda e: e.matmul(bank[DB_][:, q0:512], lhsT=onesB[:], rhs=PT[pi][:, q0:512],
                                              start=(kt == 0), stop=(kt == nkt - 1)),
                 reads=[const_t, PT_t[pi]], writes=[bank_t[DB_]])

        for kt in range(nkt):
            j = kt - 4 * c
            q0 = 128 * j if j > 0 else 0
            sbk = (SB_, SB2_)[kt % 2]
            ti = kt % 2
            P.op("tensor", lambda e, kt=kt, q0=q0, sbk=sbk: e.matmul(bank[sbk][:, q0:512], lhsT=kT[:, kt * 128:(kt + 1) * 128],
                                                                    rhs=qc[0][:, q0:512], start=True, stop=False),
                 reads=[kT_t, qc_t[0]], writes=[bank_t[sbk]])
            P.op("tensor", lambda e, q0=q0, sbk=sbk: e.matmul(bank[sbk][:, q0:512], lhsT=sel8[:, hb * 128:(hb + 1) * 128],
                                                              rhs=hi128[:, c * 512 + q0:(c + 1) * 512], start=False, stop=True),
                 reads=[Fb_t[0], big_t[1]], writes=[bank_t[sbk]])
            pi = nxt("pt", 3)
            nb_ = Ftok[:, kt * 8 + hb:kt * 8 + hb + 1]
            if j >= 0:
                P.op("vector", lambda e, q0=q0, ti=ti, sbk=sbk: e.tensor_tensor(out=tS[ti][:, 0:128], in0=bank[sbk][:, q0:q0 + 128],
                                                                                in1=maskT[:], op=ALU.add),
                     reads=[bank_t[sbk], const_t], writes=[tS_t[ti]])
                P.op("scalar", lambda e, pi=pi, q0=q0, ti=ti, nb_=nb_: e.activation(
                    out=PT[pi][:, q0:q0 + 128], in_=tS[ti][:, 0:128], func=AF.Exp, bias=nb_),
                    reads=[tS_t[ti], Ftok_t], writes=[PT_t[pi]])
                if q0 + 128 < 512:
                    P.op("scalar", lambda e, pi=pi, q0=q0, sbk=sbk, nb_=nb_: e.activation(
                        out=PT[pi][:, q0 + 128:512], in_=bank[sbk][:, q0 + 128:512], func=AF.Exp, bias=nb_),
                        reads=[bank_t[sbk], Ftok_t], writes=[PT_t[pi]])
            else:
                P.op("scalar", lambda e, pi=pi, sbk=sbk, nb_=nb_: e.activation(
                    out=PT[pi][:], in_=bank[sbk][:], func=AF.Exp, bias=nb_),
                    reads=[bank_t[sbk], Ftok_t], writes=[PT_t[pi]])
            if pend is not None:
                pv(*pend)
            pend = (kt, pi, q0)
        pv(*pend)
        P.op("vector", lambda e: e.reciprocal(out=zt[0][:], in_=bank[DB_][:]), reads=[bank_t[DB_]], writes=[zt_t[0]])
        P.op("vector", lambda e: e.tensor_tensor(out=zt[0][:], in0=bank[OB_][:], in1=zt[0][:], op=ALU.mult),
             reads=[bank_t[OB_], zt_t[0]], writes=[zt_t[0]])
        P.op("vector", lambda e: e.tensor_tensor(out=ybc[0][:], in0=zt[0][:], in1=gc[0][:], op=ALU.mult),
             reads=[zt_t[0], gc_t[0]], writes=[ybc_t[0]])
        yt = T("ys")
        ys_t[(hb, c)] = yt
        P.op("sync", lambda e: e.dma_start(out=ys_dst(hb, c), in_=ybc[0][:]),
             reads=[ybc_t[0]], writes=[yt], dma=True)
        if c % 4 == 3:
            exchange(hb, c // 4)

    for hb in range(8):
        t0 = tile_box[0]
        tile_box[0] += 4
        wsA = get_w(t0 * 128, 256)
        wsB = get_w((t0 + 2) * 128, 256)
        for c in range(NCH):
            b_chunk(hb, t0, wsA, wsB, c)

    if upto <= 3:
        return finish()
    num, den = big[0], big[1]

    def rope(dil, src_ap, src_t, dst_ap, dst_t):
        P.op("scalar", lambda e: e.copy(out=rt[0][0:64, :], in_=src_ap[64:128, :]), reads=[src_t], writes=[rt_t[0]])
        P.op("scalar", lambda e: e.copy(out=rt[0][64:128, :], in_=src_ap[0:64, :]), reads=[src_t], writes=[rt_t[0]])
        P.op("vector", lambda e: e.tensor_tensor(out=rt[0], in0=rt[0], in1=cst[0][:, 1, :], op=ALU.mult),
             reads=[rt_t[0], cst_t[0]], writes=[rt_t[0]])
        P.op("vector", lambda e: e.tensor_tensor(out=src_ap, in0=src_ap, in1=cst[0][:, 0, :], op=ALU.mult),
             reads=[src_t, cst_t[0]], writes=[src_t])
        P.op("vector", lambda e: e.tensor_tensor(
            out=dst_ap, in0=src_ap.rearrange("p (m r) -> p m r", r=dil),
            in1=rt[0].rearrange("p (m r) -> p m r", r=dil), op=ALU.add),
            reads=[src_t, rt_t[0]], writes=[dst_t])

    pendA = [None]

    def flushA():
        if pendA[0] is not None:
            pendA[0]()
            pendA[0] = None

    ablk = [0]

    def a_block(g, dil, nblk, kcm, qpm, r, nbl, n):
        ms = [m for m in (n - 1, n) if m >= 0]
        bi = ablk[0]
        ablk[0] += 1
        sbk = (SB_, SB2_)[bi % 2]
        odb = (OB_, DB_)[bi % 2]
        pi = bi % 3
        nm = len(ms)
        for mi, m in enumerate(ms):
            P.op("tensor", lambda e, m=m, mi=mi: e.matmul(
                bank[sbk][:, mi * 128:(mi + 1) * 128], lhsT=kcm[:, r, m * 128:(m + 1) * 128],
                rhs=qpm[:, r, nbl * 128:(nbl + 1) * 128], start=True, stop=True),
                reads=[kT_t, qp_t], writes=[bank_t[sbk]])
        P.op("scalar", lambda e: e.activation(out=PT[pi][:, 0:nm * 128], in_=bank[sbk][:, 0:nm * 128], func=AF.Exp, scale=SCALE),
             reads=[bank_t[sbk]], writes=[PT_t[pi]])
        mo = 0 if nm == 2 else 128
        P.op("vector", lambda e: e.tensor_tensor(out=PT[pi][:, 0:nm * 128], in0=PT[pi][:, 0:nm * 128],
                                                 in1=maskA[:, mo:mo + nm * 128], op=ALU.mult),
             reads=[PT_t[pi], const_t], writes=[PT_t[pi]])

        def second():
            for mi, m in enumerate(ms):
                P.op("tensor", lambda e, m=m, mi=mi: e.matmul(
                    bank[odb][:, 0:128], lhsT=Vt[:, r * nblk + m, :], rhs=PT[pi][:, mi * 128:(mi + 1) * 128],
                    start=(mi == 0), stop=(mi == nm - 1)),
                    reads=[Vt_t, PT_t[pi]], writes=[bank_t[odb]])
            for mi, m in enumerate(ms):
                P.op("tensor", lambda e, mi=mi: e.matmul(
                    bank[odb][:, 128:256], lhsT=onesB[:], rhs=PT[pi][:, mi * 128:(mi + 1) * 128],
                    start=(mi == 0), stop=(mi == nm - 1)),
                    reads=[const_t, PT_t[pi]], writes=[bank_t[odb]])
            numv = num[:].rearrange("p (l r) -> p r l", r=dil)[:, r, n * 128:(n + 1) * 128]
            denv = den[:].rearrange("p (l r) -> p r l", r=dil)[:, r, n * 128:(n + 1) * 128]
            if g == 0:
                P.op("vector", lambda e: e.tensor_copy(out=numv, in_=bank[odb][:, 0:128]),
                     reads=[bank_t[odb]], writes=[big_t[0]])
                P.op("vector", lambda e: e.tensor_copy(out=denv, in_=bank[odb][:, 128:256]),
                     reads=[bank_t[odb]], writes=[big_t[1]])
            else:
                P.op("vector", lambda e: e.tensor_tensor(out=numv, in0=bank[odb][:, 0:128], in1=numv, op=ALU.add),
                     reads=[bank_t[odb], big_t[0]], writes=[big_t[0]])
                P.op("vector", lambda e: e.tensor_tensor(out=denv, in0=bank[odb][:, 128:256], in1=denv, op=ALU.add),
                     reads=[bank_t[odb], big_t[1]], writes=[big_t[1]])

        flushA()
        pendA[0] = second

    def a_combine(ha, c):
        cs_ = slice(c * 512, (c + 1) * 512)
        gs_ = slice((c % 4) * 512, (c % 4 + 1) * 512)
        P.op("vector", lambda e: e.reciprocal(out=den[:, cs_], in_=den[:, cs_]), reads=[big_t[1]], writes=[big_t[1]])
        P.op("vector", lambda e: e.tensor_tensor(out=num[:, cs_], in0=num[:, cs_], in1=den[:, cs_], op=ALU.mult),
             reads=[big_t[0], big_t[1]], writes=[big_t[0]])
        P.op("vector", lambda e: e.tensor_tensor(out=ybc[0][:], in0=num[:, cs_], in1=gA[:, gs_], op=ALU.mult),
             reads=[big_t[0], gA_t], writes=[ybc_t[0]])
        yt = T("ys")
        ys_t[(8 + ha, c)] = yt
        P.op("sync", lambda e: e.dma_start(out=ys_dst(8 + ha, c), in_=ybc[0][:]),
             reads=[ybc_t[0]], writes=[yt], dma=True)
        if c % 4 == 3:
            exchange(8 + ha, c // 4)

    def a_group(ha, g, t0):
        dil, per = DILS[g], PERIOD[g]
        L = S // dil
        W = 512 // dil
        LP = per * W
        NB = LP // 128
        nblk = L // 128
        ntile = 4 if g == 2 else 3
        wsA = get_w(t0 * 128, 256)
        wsB = get_w((t0 + 2) * 128, 128 * (ntile - 2))
        kcm = kT.rearrange("p (r l) -> p r l", r=dil)
        qpm = qp[:, 0:dil * LP].rearrange("p (r l) -> p r l", r=dil)
        vpm = vTp[:, 0:dil * LP].rearrange("p (r l) -> p r l", r=dil)

        def a_chunk(c):
            hs = get_h(c)
            if c == 4 and t0 + ntile < NT2:
                prefetch_w((t0 + ntile) * 128, 256)
            cc = c % per
            pi_ = c // per
            cs_ = slice(c * 512, (c + 1) * 512)
            P.op("sync", lambda e: e.dma_start(out=cst[0], in_=cs_d[:, :, cs_].rearrange("a p t -> p a t")),
                 writes=[cst_t[0]], dma=True)
            bk = mm_tile(wsA, 0, hs)
            qk_norm(bk, t0, 2 + g, qn[0], qn_t[0])
            rope(dil, qn[0], qn_t[0], qpm[:, :, cc * W:(cc + 1) * W].rearrange("p r m -> p m r"), qp_t)
            bk = mm_tile(wsA, 1, hs)
            qk_norm(bk, t0 + 1, 5 + g, qn[0], qn_t[0])
            rope(dil, qn[0], qn_t[0], kcm[:, :, c * W:(c + 1) * W].rearrange("p r m -> p m r"), kT_t)
            bkv = mm_tile(wsB, 0, hs)
            P.op("scalar", lambda e: e.activation(
                out=vpm[:, :, cc * W:(cc + 1) * W].rearrange("p r m -> p m r"),
                in_=bank[bkv][:].rearrange("p (m r) -> p m r", r=dil), func=AF.Identity, bias=b2T[:, t0 + 2:t0 + 3]),
                reads=[bank_t[bkv], const_t], writes=[vTp_t])
            if g == 2:
                bkg = mm_tile(wsB, 1, hs)
                gs_ = slice(cc * 512, (cc + 1) * 512)
                P.op("scalar", lambda e: e.activation(out=gA[:, gs_], in_=bank[bkg][:], func=AF.Silu, bias=b2T[:, t0 + 3:t0 + 4]),
                     reads=[bank_t[bkg], const_t], writes=[gA_t])
            if cc != per - 1:
                return
            nb0 = pi_ * NB
            for r in range(dil):
                for nbl in range(NB):
                    n = nb0 + nbl
                    P.op("tensor", lambda e, r=r, nbl=nbl: e.transpose(
                        out=pbB[:, 0:128], in_=vpm[:, r, nbl * 128:(nbl + 1) * 128], identity=identB[:]),
                        reads=[vTp_t, const_t], writes=[pbB_t])
                    P.op("vector", lambda e, r=r, n=n: e.tensor_copy(out=Vt[:, r * nblk + n, :], in_=pbB[:, 0:128]),
                         reads=[pbB_t], writes=[Vt_t])
            for r in range(dil):
                for nbl in range(NB):
                    a_block(g, dil, nblk, kcm, qpm, r, nbl, nb0 + nbl)
            flushA()
            if g == 2:
                for c2_ in range(4 * pi_, 4 * pi_ + 4):
                    a_combine(ha, c2_)

        for c in range(NCH):
            a_chunk(c)

    P.op("vector", lambda e: e.memset(rt[0][:, 0:1], 0.0), reads=[const_t],
         writes=[a2_t, Fb_t[0], tS_t[0], tS_t[1], cst_t[0], qn_t[0], rt_t[0]])
    for ha in range(4):
        for g in range(3):
            t0 = tile_box[0]
            tile_box[0] += (4 if g == 2 else 3)
            a_group(ha, g, t0)
    assert tile_box[0] == NT2

    if upto <= 4:
        return finish()
    wab_v = wab_d.rearrange("(k p) n -> p k n", p=128)
    mTb = [big[0][:].bitcast(BF16).rearrange("p (j t) -> p j t", t=512),
           big[1][:].bitcast(BF16).rearrange("p (j t) -> p j t", t=512)]
    a2all_t = [a2_t, cst_t[0], qn_t[0], rt_t[0]]

    hx_t = [T("hx0"), T("hx1")]
    w512_t = wslot_t + hx_t
    rr["w5"] = 0

    def load_w512(view, col0, kh):
        ws = nxt("w5", 5)
        P.op("gpsimd", lambda e: e.dma_start(out=w512[ws], in_=view[:, kh * 16:(kh + 1) * 16, col0:col0 + 512]),
             writes=[w512_t[ws]], dma=True)
        return ws

    def mm_tile512(wsA, wsB, t, hs):
        bk = IN[nxt("in", 2)]
        for k in range(32):
            ws = (wsA, wsB)[k // 16]
            P.op("tensor", lambda e, k=k, ws=ws: e.matmul(bank[bk][:], lhsT=w512[ws][:, k % 16, t * 128:(t + 1) * 128],
                                                          rhs=hslot[hs][:, k, :], start=(k == 0), stop=(k == 31)),
                 reads=[w512_t[ws], hslot_t[hs]], writes=[bank_t[bk]])
        return bk

    def mslice(j):
        return mTb[j // 16][:, j % 16, :], big_t[j // 16]

    def o_pair(c, hs, jp):
        wsG = [load_w512(wmg_v, jp * 512, 0), load_w512(wmg_v, jp * 512, 1)]
        P.op("gpsimd", lambda e: e.dma_start(out=wabs, in_=wab_v[:, :, jp * 256:(jp + 1) * 256]),
             writes=[wabs_t] + arena_t, dma=True)
        ka = [r * 12 + u for r in range(2) for u in range(8, 12)]
        kb = [r * 12 + u for r in range(2) for u in range(8)]
        pbanks = ((AUX, SB_), (OB_, DB_))
        for jj in range(2):
            for (ks, bk) in ((ka, pbanks[jj][0]), (kb, pbanks[jj][1])):
                for ii, kc in enumerate(ks):
                    P.op("tensor", lambda e, kc=kc, ii=ii, ks=ks, bk=bk, jj=jj: e.matmul(
                        bank[bk][:], lhsT=wabs[:, kc, jj * 128:(jj + 1) * 128], rhs=ych[:, kc, :],
                        start=(ii == 0), stop=(ii == len(ks) - 1)),
                        reads=[wabs_t, ych_t, ych_pt[kc]], writes=[bank_t[bk]])
        for jj in range(2):
            j = 2 * jp + jj
            ba, bb = pbanks[jj]
            bga = mm_tile512(wsG[0], wsG[1], 2 * jj, hs)
            bgb = mm_tile512(wsG[0], wsG[1], 2 * jj + 1, hs)
            P.op("scalar", lambda e, bga=bga, j=j: e.activation(out=zt[0][:], in_=bank[bga][:], func=AF.Sigmoid,
                                                                bias=bmgT[:, 2 * j:2 * j + 1]),
                 reads=[bank_t[bga], const_t], writes=[zt_t[0]])
            P.op("scalar", lambda e, bgb=bgb, j=j: e.activation(out=sd[0][:], in_=bank[bgb][:], func=AF.Sigmoid,
                                                                bias=bmgT[:, 2 * j + 1:2 * j + 2]),
                 reads=[bank_t[bgb], const_t], writes=[sd_t[0]])
            P.op("vector", lambda e, ba=ba: e.tensor_tensor(out=zt[0][:], in0=bank[ba][:], in1=zt[0][:], op=ALU.mult),
                 reads=[bank_t[ba], zt_t[0]], writes=[zt_t[0]])
            P.op("vector", lambda e, bb=bb: e.tensor_tensor(out=sd[0][:], in0=bank[bb][:], in1=sd[0][:], op=ALU.mult),
                 reads=[bank_t[bb], sd_t[0]], writes=[sd_t[0]])
            ms_, mt_ = mslice(j)
            P.op("vector", lambda e, ms_=ms_: e.tensor_tensor(out=ms_, in0=zt[0][:], in1=sd[0][:], op=ALU.add),
                 reads=[zt_t[0], sd_t[0]], writes=[mt_])

    def o_out(c, ip):
        wsO = load_w(wo_v, ip * 256, 256)
        for ii in range(2):
            i_ = 2 * ip + ii
            oi = i_ % 2
            P.op("sync", lambda e, oi=oi, i_=i_: e.dma_start(
                out=xo_s[oi], in_=xo_d[c * 512:(c + 1) * 512, i_ * 128:(i_ + 1) * 128].rearrange("(t p) n -> p t n", p=128)),
                writes=[xo_t[oi]] + a2all_t, dma=True)
            for k in range(32):
                ms_, mt_ = mslice(k)
                P.op("tensor", lambda e, k=k, ii=ii, ms_=ms_: e.matmul(
                    bank[OB_][:], lhsT=wslot[wsO][:, k, ii * 128:(ii + 1) * 128], rhs=ms_,
                    start=(k == 0), stop=(k == 31)),
                    reads=[wslot_t[wsO], mt_], writes=[bank_t[OB_]])
            P.op("scalar", lambda e, i_=i_: e.activation(out=zt[0][:], in_=bank[OB_][:], func=AF.Identity,
                                                         scale=gateT[:, i_:i_ + 1]),
                 reads=[bank_t[OB_], mod_t], writes=[zt_t[0]])
            for q in range(4):
                P.op("tensor", lambda e, q=q: e.transpose(out=bank[DB_][:, q * 128:(q + 1) * 128],
                                                          in_=zt[0][:, q * 128:(q + 1) * 128], identity=identF[:]),
                     reads=[zt_t[0], const_t], writes=[bank_t[DB_]])
            P.op("vector", lambda e, oi=oi: e.tensor_tensor(
                out=ob_s[oi], in0=bank[DB_][:].rearrange("p (t n) -> p t n", n=128), in1=xo_s[oi], op=ALU.add),
                reads=[bank_t[DB_], xo_t[oi]], writes=[ob_t[oi]])
            P.op("sync", lambda e, oi=oi, i_=i_: e.dma_start(
                out=out_d[c * 512:(c + 1) * 512, i_ * 128:(i_ + 1) * 128].rearrange("(t p) n -> p t n", p=128), in_=ob_s[oi]),
                reads=[ob_t[oi]], writes=[T("outd")], dma=True)

    ych_pt = [T("ychp%d" % k) for k in range(24)]
    tmp_pt = [T("tmpp%d" % k) for k in range(24)]

    def o_loads(c):
        load_h(hTo_d, "o", c, hs=0)
        tmp = hslot[1][:, 0:24, :]
        tmp_t = hslot_t[1]
        for r in range(2):
            for u in range(12):
                kc = r * 12 + u
                P.op("sync", lambda e, r=r, u=u, kc=kc: e.dma_start(
                    out=ych[:, kc, :], in_=ya_p[u][0].ap()[r * 128:(r + 1) * 128, c * 512:(c + 1) * 512]),
                    reads=[ya_t[u][0]], writes=[ych_pt[kc]] + (arena_t + [ych_t] if kc == 0 else []), dma=True)
                P.op("sync", lambda e, r=r, u=u, kc=kc: e.dma_start(
                    out=tmp[:, kc, :], in_=ya_p[u][1].ap()[r * 128:(r + 1) * 128, c * 512:(c + 1) * 512]),
                    reads=[ya_t[u][1]], writes=[tmp_pt[kc]] + ([tmp_t] + hx_t if kc == 0 else []), dma=True)

    def o_blend():
        tmp = hslot[1][:, 0:24, :]
        tmp_t = hslot_t[1]
        P.op("vector", lambda e: e.tensor_scalar(out=tmp, in0=tmp, scalar1=msk[:, 1:2], scalar2=None, op0=ALU.mult),
             reads=[tmp_t, const_t] + tmp_pt + hx_t, writes=[tmp_t] + tmp_pt + hx_t)
        P.op("vector", lambda e: e.scalar_tensor_tensor(out=ych, in0=ych, scalar=msk[:, 0:1], in1=tmp, op0=ALU.mult, op1=ALU.add),
             reads=[ych_t, tmp_t, const_t] + tmp_pt + ych_pt + hx_t, writes=[ych_t] + ych_pt + hx_t)

    def o_chunk(c):
        if c == 0:
            o_loads(0)
            o_blend()
        for jp in range(16):
            o_pair(c, 0, jp)
        if c + 1 < NCH // 2:
            o_loads(c + 1)
        for ip in range(16):
            o_out(c, ip)
        if c + 1 < NCH // 2:
            o_blend()

    for c in range(NCH // 2):
        o_chunk(c)

    return finish()


A_Q0, A_K0, A_V0, A_G0 = 0, 3072, 6144, 9216
B_Q0, B_K0, B_V0, B_G0, B_F0 = 10240, 12288, 14336, 16384, 18432
M_G0 = 18448


def _consts():
    half = 64
    inv = 10000.0 ** (-np.arange(half, dtype=np.float64) / half)
    ang = np.arange(S, dtype=np.float64)[None, :] * np.concatenate([inv, inv])[:, None]
    cos = np.cos(ang)
    sin = np.sin(ang)
    sgn = np.concatenate([-np.ones(half), np.ones(half)])[:, None]
    cs = np.stack([cos, sgn * sin]).astype(np.float32)
    identF = np.eye(128, dtype=np.float32)
    selF = np.zeros((128, 8, 128), np.float32)
    for h in range(8):
        selF[h, h, :] = 1.0
    k = np.arange(128)[:, None]
    q = np.arange(512)[None, :]
    q1 = np.arange(128)[None, :]
    maskB = np.where(q1 >= k, 0.0, NEG).astype(np.float32)
    maskA = np.concatenate([(q1 <= k), (q1 >= k)], axis=1).astype(np.float32)
    return cs, identF, selF.reshape(128, 1024), maskB, maskA


def _tm(v):
    return np.ascontiguousarray(v.reshape(-1, 128).T)


_NC = None
_UPTO = 9
_DBG = False
_RAW = False


def kernel(x, c, norm_g, w_ada, b_ada, w_in, b_in, a_q_norm, a_k_norm, b_q_norm, b_k_norm,
           w_a_out, w_b_out, w_o):
    global _NC
    if _NC is None:
        _NC = build(_UPTO, _DBG)
    nc = _NC
    f = np.float32
    x = np.asarray(x, f); c = np.asarray(c, f)
    w_in0 = np.asarray(w_in, f)[0]; b_in0 = np.asarray(b_in, f)[0]
    cs, identF, selF, maskB, maskA = _consts()
    wada = np.ascontiguousarray(np.asarray(w_ada, f)[0])
    badaT = _tm(np.asarray(b_ada, f)[0])
    gT = _tm(np.asarray(norm_g, f)[0])
    wo = np.ascontiguousarray(np.asarray(w_o, f)[0])
    wa = np.asarray(w_a_out, f)[0]; wb = np.asarray(w_b_out, f)[0]
    wab = np.concatenate([np.concatenate([wb[r * 1024:(r + 1) * 1024], wa[r * 512:(r + 1) * 512]], axis=0) for r in range(2)], axis=0)
    wab = np.ascontiguousarray(wab)
    mgcols = np.concatenate([np.concatenate([M_G0 + j * 128 + np.arange(128), M_G0 + D + j * 128 + np.arange(128)]) for j in range(32)])
    wmg = np.ascontiguousarray(w_in0[:, mgcols])
    bmgT = _tm(b_in0[mgcols])
    per_core_w = []
    for p in range(2):
        cols = []
        fc = np.concatenate([B_F0 + 8 * p + np.arange(8), np.full(120, -1)])
        cols.append(fc)
        for j in range(8):
            h = 8 * p + j
            for base in (B_Q0, B_K0, B_V0, B_G0):
                cols.append(base + h * 128 + np.arange(128))
        for j in range(4):
            h = 4 * p + j
            for g in range(3):
                for base in (A_Q0, A_K0, A_V0):
                    cols.append(base + (g * 8 + h) * 128 + np.arange(128))
                if g == 2:
                    cols.append(A_G0 + h * 128 + np.arange(128))
        cols = np.concatenate(cols)
        assert cols.shape[0] == NT2 * 128
        w2 = w_in0[:, np.maximum(cols, 0)].copy()
        w2[:, cols < 0] = 0.0
        b2 = b_in0[np.maximum(cols, 0)].copy()
        b2[cols < 0] = 0.0
        per_core_w.append((np.ascontiguousarray(w2), _tm(b2)))
    gains = np.stack([np.asarray(b_q_norm, f)[0], np.asarray(b_k_norm, f)[0],
                      np.asarray(a_q_norm, f)[0, 0], np.asarray(a_q_norm, f)[0, 1], np.asarray(a_q_norm, f)[0, 2],
                      np.asarray(a_k_norm, f)[0, 0], np.asarray(a_k_norm, f)[0, 1], np.asarray(a_k_norm, f)[0, 2]], axis=1)
    gains = np.ascontiguousarray(gains.astype(f))
    in_maps = []
    for core in range(8):
        b, p = core // 2, core % 2
        w2, b2T = per_core_w[p]
        m = np.zeros((128, 2), f)
        m[:, p] = 1.0
        in_maps.append({
            "x": np.ascontiguousarray(x[b]), "xo": np.ascontiguousarray(x[b, p * 2048:(p + 1) * 2048]),
            "cT": _tm(c[b]), "wada": wada, "badaT": badaT, "gT": gT, "w2": w2, "b2T": b2T,
            "wmg": wmg, "bmgT": bmgT, "wab": wab, "wo": wo, "gains": gains, "cs": cs, "msk": m,
            "identF": identF, "selF": selF, "maskT": maskB, "maskA": maskA,
        })
    if _RAW:
        return run_bass_kernel_spmd(nc, in_maps, core_ids=list(range(8)))
    res = run_bass_kernel_spmd(nc, in_maps, core_ids=list(range(8)))
    out = np.empty((4, S, D), f)
    for core in range(8):
        b, p = core // 2, core % 2
        out[b, p * 2048:(p + 1) * 2048] = res.results[core]["out"]
    return out
```
